# Optimizing a Trainium2 kernel written in Bass

```python
import math
import jax, jax.numpy as jnp
from jax import lax
import numpy as np

D_MODEL = 1024
BATCH = 16
SEQ = 2048
DEPTH = 4

HEAD_DIM = 64
SB_HEADS = 4
MLA_HEADS = 4
RWKV_HEADS = 8
SB_WIDTH = SB_HEADS * HEAD_DIM
MLA_WIDTH = MLA_HEADS * HEAD_DIM
RWKV_WIDTH = RWKV_HEADS * HEAD_DIM
MIX_WIDTH = SB_WIDTH + MLA_WIDTH + RWKV_WIDTH

MLA_Q_LORA = 192
MLA_KV_LORA = 128
MLA_NOPE_DIM = 64
MLA_ROPE_DIM = 32
MLA_V_DIM = HEAD_DIM
ROPE_THETA = 10000.0

RWKV_DECAY_LORA = 64
RWKV_AAA_LORA = 64
RWKV_GATE_LORA = 128
RWKV_GN_EPS = 64e-5

SB_COLS = 3 * SB_WIDTH
MLA_COLS = MLA_Q_LORA + MLA_KV_LORA + MLA_ROPE_DIM
RWKV_COLS = 3 * RWKV_WIDTH + RWKV_DECAY_LORA + RWKV_AAA_LORA + RWKV_GATE_LORA
IN_COLS = SB_COLS + MLA_COLS + RWKV_COLS

D_FF = 2816
CONV_WIDTH = 3

Q_BLOCK = 128
LN_EPS = 1e-5
RMS_EPS = 1e-6
DEEPNORM_ALPHA = (2 * DEPTH) ** 0.25
DEEPNORM_BETA = (8 * DEPTH) ** -0.25

kernel_name = "hymba_style_sb_mla_rwkv7_deepnorm_convffn"


def layer_norm(x, g, b):
    xf = x.astype(jnp.float32)
    mu = jnp.mean(xf, axis=-1, keepdims=True)
    var = jnp.mean(jnp.square(xf - mu), axis=-1, keepdims=True)
    return ((xf - mu) * lax.rsqrt(var + LN_EPS) * g + b).astype(x.dtype)


def rms_norm(x, g):
    xf = x.astype(jnp.float32)
    return (xf * lax.rsqrt(jnp.mean(jnp.square(xf), axis=-1, keepdims=True) + RMS_EPS) * g).astype(x.dtype)


def split_heads(t, n_heads):
    b, s, _ = t.shape
    return t.reshape(b, s, n_heads, -1).transpose(0, 2, 1, 3)


def merge_heads(t):
    b, h, s, d = t.shape
    return t.transpose(0, 2, 1, 3).reshape(b, s, h * d)


def query_block_map(block_fn, q):
    b, h, s, d = q.shape
    nb = s // Q_BLOCK
    qb = q.reshape(b, h, nb, Q_BLOCK, d).transpose(2, 0, 1, 3, 4)
    starts = jnp.arange(nb, dtype=jnp.int32) * Q_BLOCK
    out = lax.map(lambda a: block_fn(a[0], a[1]), (qb, starts))
    return out.transpose(1, 2, 0, 3, 4).reshape(b, h, s, -1)


def stick_breaking_attention(q, k, v):
    s_len = k.shape[2]
    scale = q.shape[-1] ** -0.5
    kpos = jnp.arange(s_len, dtype=jnp.int32)

    def block(qb, start):
        z = jnp.einsum('bhqd,bhkd->bhqk', qb, k).astype(jnp.float32) * scale
        qpos = start + jnp.arange(Q_BLOCK, dtype=jnp.int32)
        past = kpos[None, :] < qpos[:, None]
        log_keep = jnp.where(past, jax.nn.log_sigmoid(-z), 0.0)
        log_later = lax.cumsum(log_keep, axis=3, reverse=True) - log_keep
        w = jnp.where(past, jnp.exp(jax.nn.log_sigmoid(z) + log_later), 0.0)
        return jnp.einsum('bhqk,bhkd->bhqd', w.astype(v.dtype), v)

    return query_block_map(block, q)


def causal_softmax_attention(q, k, v, scale):
    s_len = k.shape[2]
    kpos = jnp.arange(s_len, dtype=jnp.int32)
    neg = jnp.finfo(jnp.float32).min

    def block(qb, start):
        s = jnp.einsum('bhqd,bhkd->bhqk', qb, k).astype(jnp.float32) * scale
        qpos = start + jnp.arange(Q_BLOCK, dtype=jnp.int32)
        s = jnp.where(kpos[None, :] <= qpos[:, None], s, neg)
        p = jax.nn.softmax(s, axis=-1)
        return jnp.einsum('bhqk,bhkd->bhqd', p.astype(v.dtype), v)

    return query_block_map(block, q)


def rope_tables(s_len, dim):
    inv_freq = 1.0 / (ROPE_THETA ** (jnp.arange(0, dim, 2, dtype=jnp.float32) / dim))
    ang = jnp.arange(s_len, dtype=jnp.float32)[:, None] * inv_freq[None, :]
    return jnp.cos(ang), jnp.sin(ang)


def apply_rope(t, cos, sin):
    t1, t2 = jnp.split(t, 2, axis=-1)
    cos = cos.astype(t.dtype)
    sin = sin.astype(t.dtype)
    return jnp.concatenate([t1 * cos - t2 * sin, t2 * cos + t1 * sin], axis=-1)


def mla_mixer(cols, q_norm, w_uq, kv_norm, w_ukv):
    c_q, c_kv, k_rope = jnp.split(cols, [MLA_Q_LORA, MLA_Q_LORA + MLA_KV_LORA], axis=-1)
    q = split_heads(rms_norm(c_q, q_norm) @ w_uq, MLA_HEADS)
    kv = split_heads(rms_norm(c_kv, kv_norm) @ w_ukv, MLA_HEADS)
    q_nope, q_rope = jnp.split(q, [MLA_NOPE_DIM], axis=-1)
    k_nope, v = jnp.split(kv, [MLA_NOPE_DIM], axis=-1)
    cos, sin = rope_tables(cols.shape[1], MLA_ROPE_DIM)
    q_rope = apply_rope(q_rope, cos, sin)
    k_rope = apply_rope(k_rope[:, None], cos, sin)
    q = jnp.concatenate([q_nope, q_rope], axis=-1)
    k = jnp.concatenate([k_nope, jnp.broadcast_to(k_rope, k_nope.shape[:-1] + (MLA_ROPE_DIM,))], axis=-1)
    out = causal_softmax_attention(q, k, v, (MLA_NOPE_DIM + MLA_ROPE_DIM) ** -0.5)
    return merge_heads(out)


def token_shift_mix(p, mu):
    prev = jnp.pad(p, ((0, 0), (1, 0), (0, 0)))[:, :-1]
    return p + (prev - p) * mu


def rwkv7_mixer(cols, mu, w0, w2, a0, a2, g2, k_k, k_a, r_k, gn_g, gn_b):
    b, s, _ = cols.shape
    p = token_shift_mix(cols, mu)
    idx = [RWKV_WIDTH, 2 * RWKV_WIDTH, 3 * RWKV_WIDTH,
           3 * RWKV_WIDTH + RWKV_DECAY_LORA, 3 * RWKV_WIDTH + RWKV_DECAY_LORA + RWKV_AAA_LORA]
    r, k, v, wd, ad, gd = jnp.split(p, idx, axis=-1)
    w = -jax.nn.softplus(-(w0 + jnp.tanh(wd) @ w2)) - 0.5
    decay = jnp.exp(-jnp.exp(w.astype(jnp.float32)))
    a = jax.nn.sigmoid(a0 + ad @ a2)
    g = jax.nn.sigmoid(gd) @ g2
    kk = (k * k_k).reshape(b, s, RWKV_HEADS, HEAD_DIM).astype(jnp.float32)
    kk = kk * lax.rsqrt(jnp.maximum(jnp.sum(jnp.square(kk), axis=-1, keepdims=True), 1e-12))
    k = k * (1.0 + (a - 1.0) * k_a)

    def heads(t):
        return t.reshape(b, s, RWKV_HEADS, HEAD_DIM).astype(jnp.float32)

    rh, kh, vh, ah, wh = heads(r), heads(k), heads(v), heads(a), heads(decay)
    tm = lambda t: jnp.swapaxes(t, 0, 1)

    def step(state, inp):
        r_t, w_t, k_t, v_t, kk_t, a_t = inp
        sa = jnp.einsum('bhvk,bhk->bhv', state, -kk_t)
        state = (state * w_t[:, :, None, :] + sa[..., None] * (kk_t * a_t)[:, :, None, :]
                 + v_t[..., None] * k_t[:, :, None, :])
        return state, jnp.einsum('bhvk,bhk->bhv', state, r_t)

    init = jnp.zeros((b, RWKV_HEADS, HEAD_DIM, HEAD_DIM), jnp.float32)
    _, ys = lax.scan(step, init, (tm(rh), tm(wh), tm(kh), tm(vh), tm(kk), tm(ah)))
    y = jnp.swapaxes(ys, 0, 1)
    mu_y = jnp.mean(y, axis=-1, keepdims=True)
    var_y = jnp.mean(jnp.square(y - mu_y), axis=-1, keepdims=True)
    y = ((y - mu_y) * lax.rsqrt(var_y + RWKV_GN_EPS)).reshape(b, s, RWKV_WIDTH) * gn_g + gn_b
    bonus = jnp.sum(rh * kh * r_k, axis=-1, keepdims=True) * vh
    y = (y + bonus.reshape(b, s, RWKV_WIDTH)) * g
    return y.astype(cols.dtype)


def causal_depthwise_conv(u, w, bias):
    out = lax.conv_general_dilated(u, w[:, None, :], window_strides=(1,),
                                   padding=[(CONV_WIDTH - 1, 0)],
                                   dimension_numbers=('NWC', 'WIO', 'NWC'),
                                   feature_group_count=u.shape[-1])
    return out + bias


def hybrid_layer(x, w_in, mla_q_norm, mla_w_uq, mla_kv_norm, mla_w_ukv,
                 rwkv_mu, rwkv_w0, rwkv_w2, rwkv_a0, rwkv_a2, rwkv_g2, rwkv_k_k, rwkv_k_a, rwkv_r_k,
                 rwkv_gn_g, rwkv_gn_b, w_o, ln1_g, ln1_b,
                 ffn_w_up, ffn_conv_w, ffn_conv_b, ffn_w_down, ln2_g, ln2_b):
    h = x @ w_in
    sb_cols, mla_cols, rwkv_cols = jnp.split(h, [SB_COLS, SB_COLS + MLA_COLS], axis=-1)
    q, k, v = (split_heads(t, SB_HEADS) for t in jnp.split(sb_cols, 3, axis=-1))
    o_sb = merge_heads(stick_breaking_attention(q, k, v))
    o_mla = mla_mixer(mla_cols, mla_q_norm, mla_w_uq, mla_kv_norm, mla_w_ukv)
    o_rwkv = rwkv7_mixer(rwkv_cols, rwkv_mu, rwkv_w0, rwkv_w2, rwkv_a0, rwkv_a2, rwkv_g2,
                         rwkv_k_k, rwkv_k_a, rwkv_r_k, rwkv_gn_g, rwkv_gn_b)
    mix = jnp.concatenate([o_sb, o_mla, o_rwkv], axis=-1) @ w_o
    x = layer_norm(DEEPNORM_ALPHA * x + mix, ln1_g, ln1_b)
    u_act, u_gate = jnp.split(x @ ffn_w_up, 2, axis=-1)
    hid = jax.nn.gelu(causal_depthwise_conv(u_act, ffn_conv_w, ffn_conv_b), approximate=False) * u_gate
    x = layer_norm(DEEPNORM_ALPHA * x + hid @ ffn_w_down, ln2_g, ln2_b)
    return x


def setup_inputs(seed: int = 0) -> dict:
    key = jax.random.key(seed)
    ks = iter(jax.random.split(key, 32))
    L = DEPTH

    def nrm(shape, scale):
        return jax.random.normal(next(ks), shape, jnp.float32) * scale

    def unif(shape, lo, hi):
        return jax.random.uniform(next(ks), shape, jnp.float32, lo, hi)

    return {
        "x": nrm((BATCH, SEQ, D_MODEL), 1.0),
        "w_in": nrm((L, D_MODEL, IN_COLS), D_MODEL ** -0.5),
        "mla_q_norm": 1.0 + nrm((L, MLA_Q_LORA), 0.02),
        "mla_w_uq": nrm((L, MLA_Q_LORA, MLA_HEADS * (MLA_NOPE_DIM + MLA_ROPE_DIM)), MLA_Q_LORA ** -0.5),
        "mla_kv_norm": 1.0 + nrm((L, MLA_KV_LORA), 0.02),
        "mla_w_ukv": nrm((L, MLA_KV_LORA, MLA_HEADS * (MLA_NOPE_DIM + MLA_V_DIM)), MLA_KV_LORA ** -0.5),
        "rwkv_mu": unif((L, RWKV_COLS), 0.0, 1.0),
        "rwkv_w0": unif((L, RWKV_WIDTH), -6.5, -1.5),
        "rwkv_w2": nrm((L, RWKV_DECAY_LORA, RWKV_WIDTH), 0.1),
        "rwkv_a0": nrm((L, RWKV_WIDTH), 0.1),
        "rwkv_a2": nrm((L, RWKV_AAA_LORA, RWKV_WIDTH), RWKV_AAA_LORA ** -0.5),
        "rwkv_g2": nrm((L, RWKV_GATE_LORA, RWKV_WIDTH), RWKV_GATE_LORA ** -0.5),
        "rwkv_k_k": 0.85 + nrm((L, RWKV_WIDTH), 0.05),
        "rwkv_k_a": 1.0 + nrm((L, RWKV_WIDTH), 0.05),
        "rwkv_r_k": nrm((L, RWKV_HEADS, HEAD_DIM), 0.1),
        "rwkv_gn_g": 1.0 + nrm((L, RWKV_WIDTH), 0.02),
        "rwkv_gn_b": nrm((L, RWKV_WIDTH), 0.02),
        "w_o": nrm((L, MIX_WIDTH, D_MODEL), MIX_WIDTH ** -0.5 * DEEPNORM_BETA),
        "ln1_g": 1.0 + nrm((L, D_MODEL), 0.02),
        "ln1_b": nrm((L, D_MODEL), 0.02),
        "ffn_w_up": nrm((L, D_MODEL, 2 * D_FF), D_MODEL ** -0.5),
        "ffn_conv_w": nrm((L, CONV_WIDTH, D_FF), CONV_WIDTH ** -0.5),
        "ffn_conv_b": nrm((L, D_FF), 0.02),
        "ffn_w_down": nrm((L, D_FF, D_MODEL), D_FF ** -0.5 * DEEPNORM_BETA),
        "ln2_g": 1.0 + nrm((L, D_MODEL), 0.02),
        "ln2_b": nrm((L, D_MODEL), 0.02),
    }


def reference(x, w_in, mla_q_norm, mla_w_uq, mla_kv_norm, mla_w_ukv,
              rwkv_mu, rwkv_w0, rwkv_w2, rwkv_a0, rwkv_a2, rwkv_g2, rwkv_k_k, rwkv_k_a, rwkv_r_k,
              rwkv_gn_g, rwkv_gn_b, w_o, ln1_g, ln1_b,
              ffn_w_up, ffn_conv_w, ffn_conv_b, ffn_w_down, ln2_g, ln2_b):
    for l in range(DEPTH):
        x = hybrid_layer(x, w_in[l], mla_q_norm[l], mla_w_uq[l], mla_kv_norm[l], mla_w_ukv[l],
                         rwkv_mu[l], rwkv_w0[l], rwkv_w2[l], rwkv_a0[l], rwkv_a2[l], rwkv_g2[l],
                         rwkv_k_k[l], rwkv_k_a[l], rwkv_r_k[l], rwkv_gn_g[l], rwkv_gn_b[l],
                         w_o[l], ln1_g[l], ln1_b[l],
                         ffn_w_up[l], ffn_conv_w[l], ffn_conv_b[l], ffn_w_down[l], ln2_g[l], ln2_b[l])
    return x
```

```python
import math
import os
from contextlib import ExitStack

import numpy as np
import concourse.bass as bass
import concourse.mybir as mybir
from concourse.bass_utils import run_bass_kernel_spmd

F32 = mybir.dt.float32
BF16 = mybir.dt.bfloat16
AF = mybir.ActivationFunctionType
ALU = mybir.AluOpType
AX = mybir.AxisListType

NCORES = 8
DEPTH = 4
T = 2048
NT = 16
D = 1024
NSEQ = 2
IN_COLS = 2912
DFF = 2816
NFC = 22
ALPHA = (2 * DEPTH) ** 0.25
LN_EPS = 1e-5
RMS_EPS = 1e-6
GN_EPS = 64e-5
RW0 = 1120

MU0, W0C, A0C, KKC, KAC, RKC, QNC, KVNC, CWC, CBC, NPF = 0, 14, 18, 22, 26, 30, 34, 36, 37, 103, 125
LN1G, LN1B, LN2G, LN2B, GNG, GNB, NPB = 0, 1024, 2048, 3072, 4096, 4608, 5120
C_ID, C_TRILS, C_TRIUS, C_TRILI, C_TRIUI, C_BD, C_HS, C_RM, NCST = 0, 128, 256, 384, 512, 640, 768, 770, 1282

SAME_ENG_SYNC = bool(int(os.environ.get("SES", "1")))


class Sem:
    __slots__ = ("h", "cnt", "q")

    def __init__(self, h, q):
        self.h = h
        self.cnt = 0
        self.q = q


class Dep:
    __slots__ = ("w", "r", "sem", "persist")

    def __init__(self, persist=False):
        self.w = None
        self.r = {}
        self.sem = None
        self.persist = persist


class Buf:
    def __init__(self, h, nslots=1, persist=False):
        self.h = h
        self.deps = [Dep(persist) for _ in range(nslots)]

    def d(self, i=0):
        return self.deps[i]

    def all(self):
        return list(self.deps)


class Eng:
    def __init__(self, name, e, sem):
        self.name = name
        self.e = e
        self.sem = sem
        self.cnt = 0
        self.seen = {}


class K:
    def __init__(self, nc):
        self.nc = nc
        self.eng = {}
        for n in ["tensor", "vector", "scalar", "gpsimd", "sync"]:
            self.eng[n] = Eng(n, getattr(nc, n), nc.alloc_semaphore("e_" + n))
        self.all_sems = []
        self.free_sems = {"sync": [], "gpsimd": []}
        self.phase_deps = []

    def _wait(self, es, toks):
        need = {}
        for t in toks:
            if t is None:
                continue
            kk = id(t[0])
            if kk not in need or need[kk][1] < t[1]:
                need[kk] = t
        for kk, (sem, val) in need.items():
            if es.seen.get(kk, 0) >= val:
                continue
            if sem is es.sem:
                if es.name == "tensor" or not SAME_ENG_SYNC:
                    continue
            es.e.wait_ge(sem, val)
            es.seen[kk] = val

    @staticmethod
    def _toks(r, w):
        toks = [d.w for d in r]
        for d in w:
            toks.append(d.w)
            toks.extend(d.r.values())
        return toks

    def op(self, en, fn, r=(), w=()):
        es = self.eng[en]
        self._wait(es, self._toks(r, w))
        ins = fn(es.e)
        es.cnt += 1
        ins.then_inc(es.sem, 1)
        tok = (es.sem, es.cnt)
        for d in r:
            d.r[id(es.sem)] = tok
        for d in w:
            d.w = tok
            d.r = {}
        return tok

    def _dsem(self, d0, qn):
        if d0.sem is not None and d0.sem.q != qn:
            d0.sem = None
        if d0.sem is None:
            if self.free_sems[qn]:
                d0.sem = self.free_sems[qn].pop()
            else:
                d0.sem = Sem(self.nc.alloc_semaphore("d%d" % len(self.all_sems)), qn)
                self.all_sems.append(d0.sem)
            if not d0.persist:
                self.phase_deps.append(d0)
        return d0.sem

    def dma(self, qn, out, in_, r=(), w=(), nowaw=False, semdep=None):
        es = self.eng[qn]
        if nowaw:
            self._wait(es, [d.w for d in r])
        else:
            self._wait(es, self._toks(r, w))
        sm = self._dsem(semdep if semdep is not None else w[0], qn)
        sm.cnt += 16
        es.e.dma_start(out=out, in_=in_).then_inc(sm.h, 16)
        tok = (sm.h, sm.cnt)
        for d in r:
            d.r[id(sm.h)] = tok
        for d in w:
            d.w = tok
            if not nowaw:
                d.r = {}
        return tok

    def dma_batch(self, qn, items):
        es = self.eng[qn]
        deps = [it[2] for it in items]
        self._wait(es, self._toks((), deps))
        sm = self._dsem(deps[0], qn)
        for out, in_, d in items:
            sm.cnt += 16
            es.e.dma_start(out=out, in_=in_).then_inc(sm.h, 16)
        tok = (sm.h, sm.cnt)
        for d in deps:
            d.w = tok
            d.r = {}

    def barrier(self):
        toks = [(es.sem, es.cnt) for es in self.eng.values() if es.cnt > 0]
        toks += [(sm.h, sm.cnt) for sm in self.all_sems if sm.cnt > 0]
        for es in self.eng.values():
            self._wait(es, toks)

    def end_phase(self):
        self.barrier()
        for d in self.phase_deps:
            if d.sem is not None:
                self.free_sems[d.sem.q].append(d.sem)
                d.sem = None
        self.phase_deps = []


class Ctx:
    pass


_UID = [0]


def _uname(name):
    _UID[0] += 1
    return "%s_%d" % (name, _UID[0])


def sb(st, nc, name, shape, dt, nslots=1):
    return Buf(st.enter_context(nc.sbuf_tensor(_uname(name), shape, dt)), nslots)


def ps(st, nc, name, shape, dt=F32, nslots=1):
    return Buf(st.enter_context(nc.psum_tensor(_uname(name), shape, dt)), nslots)


def layernorm(c, t_ap, t_deps, out_ap, out_deps, g_ap, b_ap, sc, b, gdeps=()):
    k = c.k
    stats, mv, rs = sc["stats"], sc["mv"], sc["rs"]
    for hf in range(2):
        k.op("vector", lambda e: e.bn_stats(out=stats.h[:, b, hf * 6:(hf + 1) * 6],
                                            in_=t_ap[:, hf * 512:(hf + 1) * 512]),
             r=t_deps, w=[stats.d(b)])
    k.op("vector", lambda e: e.bn_aggr(out=mv.h[:, b, :], in_=stats.h[:, b, :]),
         r=[stats.d(b)], w=[mv.d(b)])
    k.op("gpsimd", lambda e: e.tensor_scalar(out=rs.h[:, b, 0:1], in0=mv.h[:, b, 1:2], scalar1=LN_EPS,
                                             scalar2=None, op0=ALU.add),
         r=[mv.d(b)], w=[rs.d(b)])
    k.op("gpsimd", lambda e: e.tensor_tensor(out=rs.h[:, b, 1:2], in0=rs.h[:, b, 0:1],
                                             in1=c.mhalf.h[:, 0:1], op=ALU.pow),
         r=[rs.d(b)], w=[rs.d(b)])
    k.op("vector", lambda e: e.tensor_scalar(out=t_ap, in0=t_ap, scalar1=mv.h[:, b, 0:1],
                                             scalar2=rs.h[:, b, 1:2], op0=ALU.subtract, op1=ALU.mult),
         r=[mv.d(b), rs.d(b)] + t_deps, w=t_deps)
    k.op("gpsimd", lambda e: e.tensor_tensor(out=t_ap, in0=t_ap, in1=g_ap, op=ALU.mult),
         r=list(t_deps) + list(gdeps), w=t_deps)
    k.op("vector", lambda e: e.tensor_tensor(out=out_ap, in0=t_ap, in1=b_ap, op=ALU.add),
         r=list(t_deps) + list(gdeps), w=out_deps)


def phase_xT(c, src, s, xT):
    k, nc = c.k, c.nc
    with ExitStack() as st:
        xt = sb(st, nc, "p1_xt", [128, 3, D], F32, 3)
        pt = [ps(st, nc, "p1_ps%d" % i, [128, 8, 128], F32) for i in range(2)]
        for i in range(NT):
            sl = i % 3
            b = i % 2
            k.dma("sync", out=xt.h[:, sl, :], in_=src[s * T + i * 128:s * T + (i + 1) * 128, :], w=[xt.d(sl)])
            for kc in range(8):
                k.op("tensor", lambda e: e.matmul(pt[b].h[:, kc, :], lhsT=xt.h[:, sl, kc * 128:(kc + 1) * 128],
                                                  rhs=c.identf.h[:], is_transpose=True),
                     r=[xt.d(sl)], w=[pt[b].d()])
            en = "scalar" if i % 2 else "vector"
            if en == "scalar":
                k.op("scalar", lambda e: e.copy(out=xT.h[:, :, i * 128:(i + 1) * 128], in_=pt[b].h[:]),
                     r=[pt[b].d()], w=[])
            else:
                k.op("vector", lambda e: e.tensor_copy(out=xT.h[:, :, i * 128:(i + 1) * 128], in_=pt[b].h[:]),
                     r=[pt[b].d()], w=[])
        k.end_phase()


def phase_B(c, l, s, src, mix, x1T):
    k, nc = c.k, c.nc
    with ExitStack() as st:
        wo = sb(st, nc, "pb_wo", [128, 8, D], BF16, 8)
        k.dma_batch("gpsimd", [(wo.h[:, kc, :], c.w_o[l, kc * 128:(kc + 1) * 128, :], wo.d(kc)) for kc in range(8)])
        lnp = sb(st, nc, "pb_lnp", [128, 2048], F32)
        k.dma("sync", out=lnp.h[:], in_=c.pbc[l:l + 1, LN1G:LN1G + 2048].partition_broadcast(128), w=[lnp.d()])
        xt = sb(st, nc, "pb_xt", [128, 2, D], F32, 2)
        tt = sb(st, nc, "pb_tt", [128, 2, D], F32, 2)
        xo = sb(st, nc, "pb_xo", [128, 2, D], F32, 2)
        mixT = sb(st, nc, "pb_mixT", [128, 2, 8, 128], BF16, 2)
        sc = dict(stats=sb(st, nc, "pb_stats", [128, 2, 12], F32, 2), mv=sb(st, nc, "pb_mv", [128, 2, 2], F32, 2),
                  rs=sb(st, nc, "pb_rs", [128, 2, 2], F32, 2))
        pT = [ps(st, nc, "pb_pT%d" % i, [128, 8, 128], BF16) for i in range(2)]
        pso = [ps(st, nc, "pb_pso%d" % i, [128, D], F32, 2) for i in range(2)]
        px = ps(st, nc, "pb_px", [128, 8, 128], F32)
        for i in range(NT):
            b = i % 2
            r0 = s * T + i * 128
            k.dma("sync", out=xt.h[:, b, :], in_=src[r0:r0 + 128, :], w=[xt.d(b)])
            for kc in range(8):
                k.op("tensor", lambda e: e.matmul(pT[b].h[:, kc, :], lhsT=mix.h[:, i, kc * 128:(kc + 1) * 128],
                                                  rhs=c.identb.h[:], is_transpose=True), w=[pT[b].d()])
            k.op("scalar", lambda e: e.copy(out=mixT.h[:, b, :, :], in_=pT[b].h[:]), r=[pT[b].d()], w=[mixT.d(b)])
            for hf in range(2):
                for kc in range(8):
                    k.op("tensor", lambda e: e.matmul(pso[b].h[:, hf * 512:(hf + 1) * 512], lhsT=mixT.h[:, b, kc, :],
                                                      rhs=wo.h[:, kc, hf * 512:(hf + 1) * 512],
                                                      start=(kc == 0), stop=(kc == 7)),
                         r=[mixT.d(b), wo.d(kc)], w=[pso[b].d(hf)])
            for hf in range(2):
                k.op("vector", lambda e: e.scalar_tensor_tensor(
                    out=tt.h[:, b, hf * 512:(hf + 1) * 512], in0=xt.h[:, b, hf * 512:(hf + 1) * 512], scalar=ALPHA,
                    in1=pso[b].h[:, hf * 512:(hf + 1) * 512], op0=ALU.mult, op1=ALU.add),
                    r=[xt.d(b), pso[b].d(hf)], w=[tt.d(b)])
            layernorm(c, tt.h[:, b, :], [tt.d(b)], xo.h[:, b, :], [xo.d(b)], lnp.h[:, 0:1024], lnp.h[:, 1024:2048], sc, b, [lnp.d()])
            k.dma("sync", out=c.x1d[r0:r0 + 128, :], in_=xo.h[:, b, :], r=[xo.d(b)], w=[], semdep=xo.d(b))
            for kc in range(8):
                k.op("tensor", lambda e: e.matmul(px.h[:, kc, :], lhsT=xo.h[:, b, kc * 128:(kc + 1) * 128],
                                                  rhs=c.identf.h[:], is_transpose=True),
                     r=[xo.d(b)], w=[px.d()])
            k.op("scalar", lambda e: e.copy(out=x1T.h[:, :, i * 128:(i + 1) * 128], in_=px.h[:]), r=[px.d()], w=[])
        k.end_phase()


def phase_C(c, l, s, dst, dst_dep, x1T):
    k, nc = c.k, c.nc
    pf = c.pfm
    with ExitStack() as st:
        wd = sb(st, nc, "pc_wd", [128, NFC, D], BF16, NFC)
        k.dma_batch("gpsimd", [(wd.h[:, kc, :], c.w_down[l, kc * 128:(kc + 1) * 128, :], wd.d(kc)) for kc in range(NFC)])
        lnp = sb(st, nc, "pc_lnp", [128, 2048], F32)
        k.dma("sync", out=lnp.h[:], in_=c.pbc[l:l + 1, LN2G:LN2G + 2048].partition_broadcast(128), w=[lnp.d()])
        hid = sb(st, nc, "pc_hid", [128, NFC, 1024], BF16, NFC)
        wup = sb(st, nc, "pc_wup", [128, 2, 2, 8, 256], BF16, 4)
        ua = sb(st, nc, "pc_ua", [128, 2, 1026], F32, 2)
        cv = sb(st, nc, "pc_cv", [128, 2, 1024], F32, 2)
        halo = sb(st, nc, "pc_halo", [128, NFC, 2], F32, NFC)
        xt = sb(st, nc, "pc_xt", [128, 2, D], F32, 2)
        tt = sb(st, nc, "pc_tt", [128, 2, D], F32, 2)
        sc = dict(stats=sb(st, nc, "pc_stats", [128, 2, 12], F32, 2), mv=sb(st, nc, "pc_mv", [128, 2, 2], F32, 2),
                  rs=sb(st, nc, "pc_rs", [128, 2, 2], F32, 2))
        P = [ps(st, nc, "pc_P%d" % i, [128, 1024], F32, 2) for i in range(4)]
        w_up_v = c.w_up[l].rearrange("(kc p) c -> p kc c", p=128)
        it = 0
        gi = 0
        for hs in range(2):
            for g in range(NFC // 2):
                sl = gi % 2
                gi += 1
                for ag in range(2):
                    col0 = ag * DFF + g * 256
                    k.dma("gpsimd", out=wup.h[:, sl, ag, :, :], in_=w_up_v[:, :, col0:col0 + 256],
                          w=[wup.d(sl * 2 + ag)])
                for fi in range(2):
                    fc = g * 2 + fi
                    b = it % 2
                    it += 1
                    pa, pg = P[b * 2], P[b * 2 + 1]
                    for ag, pp in ((0, pa), (1, pg)):
                        for tb in range(2):
                            for kc in range(8):
                                t0 = hs * 1024 + tb * 512
                                k.op("tensor", lambda e: e.matmul(
                                    pp.h[:, tb * 512:(tb + 1) * 512], lhsT=wup.h[:, sl, ag, kc, fi * 128:(fi + 1) * 128],
                                    rhs=x1T.h[:, kc, t0:t0 + 512], start=(kc == 0), stop=(kc == 7)),
                                    r=[wup.d(sl * 2 + ag)], w=[pp.d(tb)])
                    k.op("scalar", lambda e: e.copy(out=ua.h[:, b, 2:1026], in_=pa.h[:]), r=pa.all(), w=[ua.d(b)])
                    if hs == 0:
                        k.op("gpsimd", lambda e: e.memset(ua.h[:, b, 0:2], 0.0), w=[ua.d(b)])
                        k.op("gpsimd", lambda e: e.tensor_copy(out=halo.h[:, fc, :], in_=ua.h[:, b, 1024:1026]),
                             r=[ua.d(b)], w=[halo.d(fc)])
                    else:
                        k.op("gpsimd", lambda e: e.tensor_copy(out=ua.h[:, b, 0:2], in_=halo.h[:, fc, :]),
                             r=[halo.d(fc)], w=[ua.d(b)])
                    k.op("vector", lambda e: e.tensor_scalar(
                        out=cv.h[:, b, :], in0=ua.h[:, b, 2:1026], scalar1=pf.h[:, CWC + 2 * NFC + fc:CWC + 2 * NFC + fc + 1],
                        scalar2=pf.h[:, CBC + fc:CBC + fc + 1], op0=ALU.mult, op1=ALU.add), r=[ua.d(b)], w=[cv.d(b)])
                    k.op("vector", lambda e: e.scalar_tensor_tensor(
                        out=cv.h[:, b, :], in0=ua.h[:, b, 1:1025], scalar=pf.h[:, CWC + NFC + fc:CWC + NFC + fc + 1],
                        in1=cv.h[:, b, :], op0=ALU.mult, op1=ALU.add), r=[ua.d(b)], w=[cv.d(b)])
                    k.op("vector", lambda e: e.scalar_tensor_tensor(
                        out=cv.h[:, b, :], in0=ua.h[:, b, 0:1024], scalar=pf.h[:, CWC + fc:CWC + fc + 1],
                        in1=cv.h[:, b, :], op0=ALU.mult, op1=ALU.add), r=[ua.d(b)], w=[cv.d(b)])
                    k.op("scalar", lambda e: e.activation(out=cv.h[:, b, :], in_=cv.h[:, b, :], func=AF.Gelu),
                         r=[cv.d(b)], w=[cv.d(b)])
                    k.op("vector", lambda e: e.tensor_tensor(out=hid.h[:, fc, :], in0=cv.h[:, b, :], in1=pg.h[:],
                                                             op=ALU.mult), r=[cv.d(b)] + pg.all(), w=[hid.d(fc)])
            for j in range(8):
                i = hs * 8 + j
                b = j % 2
                pd = P[b]
                r0 = s * T + i * 128
                k.dma("sync", out=xt.h[:, b, :], in_=c.x1d[r0:r0 + 128, :], w=[xt.d(b)])
                for hf in range(2):
                    for kc in range(NFC):
                        k.op("tensor", lambda e: e.matmul(pd.h[:, hf * 512:(hf + 1) * 512],
                                                          lhsT=hid.h[:, kc, j * 128:(j + 1) * 128],
                                                          rhs=wd.h[:, kc, hf * 512:(hf + 1) * 512],
                                                          start=(kc == 0), stop=(kc == NFC - 1)),
                             r=[hid.d(kc), wd.d(kc)], w=[pd.d(hf)])
                for hf in range(2):
                    k.op("vector", lambda e: e.scalar_tensor_tensor(
                        out=tt.h[:, b, hf * 512:(hf + 1) * 512], in0=xt.h[:, b, hf * 512:(hf + 1) * 512], scalar=ALPHA,
                        in1=pd.h[:, hf * 512:(hf + 1) * 512], op0=ALU.mult, op1=ALU.add),
                        r=[xt.d(b), pd.d(hf)], w=[tt.d(b)])
                layernorm(c, tt.h[:, b, :], [tt.d(b)], tt.h[:, b, :], [tt.d(b)], lnp.h[:, 0:1024], lnp.h[:, 1024:2048], sc, b, [lnp.d()])
                k.dma("sync", out=dst[r0:r0 + 128, :], in_=tt.h[:, b, :], r=[tt.d(b)], w=[], semdep=tt.d(b))
        k.end_phase()


def attention(c, mode, qsel, ksel, kdim, scale, v, mix, mixcol):
    k, nc = c.k, c.nc
    with ExitStack() as st:
        wb = sb(st, nc, "at_wb", [128, 2, T], BF16, 2)
        wT = sb(st, nc, "at_wT", [128, 2, T], BF16, 2)
        if mode == "sb":
            e_sb = sb(st, nc, "at_e", [128, 2, T], F32, 2)
            Fb = sb(st, nc, "at_F", [128, 2, T + 1], F32, 2)
            ones = sb(st, nc, "at_ones", [128, T], F32)
            k.op("gpsimd", lambda e: e.memset(ones.h[:], 1.0), w=[ones.d()])
        else:
            mx = sb(st, nc, "at_mx", [128, 2, 4], F32, 2)
            maskb = sb(st, nc, "at_maskb", [128, 128], BF16)
            k.op("vector", lambda e: e.tensor_copy(out=maskb.h[:], in_=c.cst.h[:, C_TRILI:C_TRILI + 128]), w=[maskb.d()])
        pz = ps(st, nc, "at_pz", [128, T], F32, 4)
        pT = ps(st, nc, "at_pT", [128, NT, 128], BF16)
        po = [ps(st, nc, "at_po%d" % i, [128, 512], F32) for i in range(2)]
        it = 0
        for h in range(4):
            for qb in range(NT):
                b = it % 2
                osl = it % 2
                it += 1
                nk = (qb + 1) * 128
                d0 = qb * 128
                nb4 = (nk + 511) // 512
                for kb4 in range(nb4):
                    n0 = kb4 * 512
                    n1 = min(nk, n0 + 512)
                    k.op("tensor", lambda e: e.matmul(pz.h[:, n0:n1], lhsT=qsel(h, d0, d0 + 128), rhs=ksel(h, n0, n1),
                                                      start=True, stop=True), w=[pz.d(kb4)])
                pzd = [pz.d(j) for j in range(nb4)]
                if mode == "sb":
                    k.op("scalar", lambda e: e.activation(out=e_sb.h[:, b, 0:nk], in_=pz.h[:, 0:nk], func=AF.Exp, scale=scale),
                         r=pzd, w=[e_sb.d(b)])
                    k.op("gpsimd", lambda e: e.tensor_tensor(out=e_sb.h[:, b, d0:nk], in0=e_sb.h[:, b, d0:nk],
                                                             in1=c.cst.h[:, C_TRILS:C_TRILS + 128], op=ALU.mult),
                         r=[e_sb.d(b)], w=[e_sb.d(b)])
                    k.op("scalar", lambda e: e.activation(out=Fb.h[:, b, 1:nk + 1], in_=e_sb.h[:, b, 0:nk], func=AF.Ln, bias=1.0),
                         r=[e_sb.d(b)], w=[Fb.d(b)])
                    k.op("vector", lambda e: e.tensor_tensor_scan(out=Fb.h[:, b, 1:nk + 1], data0=ones.h[:, 0:nk],
                                                                  data1=Fb.h[:, b, 1:nk + 1], initial=0.0,
                                                                  op0=ALU.mult, op1=ALU.subtract),
                         r=[ones.d(), Fb.d(b)], w=[Fb.d(b)])
                    k.op("gpsimd", lambda e: e.memset(Fb.h[:, b, 0:1], 0.0), r=[Fb.d(b)], w=[Fb.d(b)])
                    k.op("scalar", lambda e: e.activation(out=Fb.h[:, b, 0:nk], in_=Fb.h[:, b, 0:nk], func=AF.Exp, scale=-1.0,
                                                          bias=Fb.h[:, b, nk:nk + 1]),
                         r=[Fb.d(b)], w=[Fb.d(b)])
                    k.op("vector", lambda e: e.tensor_tensor(out=wb.h[:, b, 0:nk], in0=e_sb.h[:, b, 0:nk], in1=Fb.h[:, b, 0:nk],
                                                             op=ALU.mult), r=[e_sb.d(b), Fb.d(b)], w=[wb.d(b)])
                else:
                    k.op("vector", lambda e: e.tensor_reduce(out=mx.h[:, b, 0:1], in_=pz.h[:, 0:nk], axis=AX.X, op=ALU.max),
                         r=pzd, w=[mx.d(b)])
                    k.op("gpsimd", lambda e: e.tensor_scalar(out=mx.h[:, b, 1:2], in0=mx.h[:, b, 0:1], scalar1=-scale, scalar2=None,
                                                             op0=ALU.mult), r=[mx.d(b)], w=[mx.d(b)])
                    k.op("scalar", lambda e: e.activation(out=wb.h[:, b, 0:nk], in_=pz.h[:, 0:nk], func=AF.Exp, scale=scale,
                                                          bias=mx.h[:, b, 1:2]), r=pzd + [mx.d(b)], w=[wb.d(b)])
                    k.op("gpsimd", lambda e: e.tensor_tensor(out=wb.h[:, b, d0:nk], in0=wb.h[:, b, d0:nk], in1=maskb.h[:],
                                                             op=ALU.mult), r=[wb.d(b), maskb.d()], w=[wb.d(b)])
                    k.op("vector", lambda e: e.tensor_reduce(out=mx.h[:, b, 2:3], in_=wb.h[:, b, 0:nk], axis=AX.X, op=ALU.add),
                         r=[wb.d(b)], w=[mx.d(b)])
                    k.op("vector", lambda e: e.reciprocal(out=mx.h[:, b, 3:4], in_=mx.h[:, b, 2:3]), r=[mx.d(b)], w=[mx.d(b)])
                for kb in range(qb + 1):
                    k.op("tensor", lambda e: e.matmul(pT.h[:, kb, :], lhsT=wb.h[:, b, kb * 128:(kb + 1) * 128], rhs=c.identb.h[:],
                                                      is_transpose=True), r=[wb.d(b)], w=[pT.d()])
                if it % 2:
                    k.op("scalar", lambda e: e.copy(out=wT.h[:, b, 0:nk], in_=pT.h[:, 0:qb + 1, :]), r=[pT.d()], w=[wT.d(b)])
                else:
                    k.op("vector", lambda e: e.tensor_copy(out=wT.h[:, b, 0:nk], in_=pT.h[:, 0:qb + 1, :]), r=[pT.d()], w=[wT.d(b)])
                for kb in range(qb + 1):
                    k.op("tensor", lambda e: e.matmul(po[osl].h[:, 0:64], lhsT=wT.h[:, b, kb * 128:(kb + 1) * 128],
                                                      rhs=v.h[:, kb, h * 64:(h + 1) * 64], start=(kb == 0), stop=(kb == qb)),
                         r=[wT.d(b)], w=[po[osl].d()])
                mc = mixcol + h * 64
                if mode == "sb":
                    k.op("scalar", lambda e: e.copy(out=mix.h[:, qb, mc:mc + 64], in_=po[osl].h[:, 0:64]), r=[po[osl].d()], w=[])
                else:
                    k.op("vector", lambda e: e.tensor_scalar(out=mix.h[:, qb, mc:mc + 64], in0=po[osl].h[:, 0:64],
                                                             scalar1=mx.h[:, b, 3:4], scalar2=None, op0=ALU.mult),
                         r=[po[osl].d(), mx.d(b)], w=[])
        k.end_phase()


def proj_fm(c, pp, n, w_ap_fn, M, xT, out_fn, r=()):
    k = c.k
    for tb in range(4):
        p = pp[n[0] % len(pp)]
        n[0] += 1
        for kc in range(8):
            k.op("tensor", lambda e: e.matmul(p.h[0:M, :], lhsT=w_ap_fn(kc), rhs=xT.h[:, kc, tb * 512:(tb + 1) * 512],
                                              start=(kc == 0), stop=(kc == 7)), r=list(r), w=[p.d()])
        out_fn(tb, p)


def evac(c, n, out_ap, in_ap, r, w=()):
    k = c.k
    if n % 2:
        k.op("scalar", lambda e: e.copy(out=out_ap, in_=in_ap), r=r, w=list(w))
    else:
        k.op("vector", lambda e: e.tensor_copy(out=out_ap, in_=in_ap), r=r, w=list(w))


def mixer_sb(c, l, s, xT, mix):
    k, nc = c.k, c.nc
    with ExitStack() as st:
        w = sb(st, nc, "sb_w", [128, 8, 768], BF16)
        w_in_v = c.w_in[l].rearrange("(kc p) c -> p kc c", p=128)
        k.dma("gpsimd", out=w.h[:], in_=w_in_v[:, :, 0:768], w=[w.d()])
        qk = sb(st, nc, "sb_qk", [128, 4, T], BF16)
        v = sb(st, nc, "sb_v", [128, NT, 256], BF16)
        with ExitStack() as st2:
            pp = [ps(st2, nc, "sb_pp%d" % i, [128, 512], F32) for i in range(4)]
            n = [0]
            for ci in range(4):
                proj_fm(c, pp, n, lambda kc: w.h[:, kc, ci * 128:(ci + 1) * 128], 128, xT,
                        lambda tb, p: evac(c, n[0], qk.h[:, ci, tb * 512:(tb + 1) * 512], p.h[:], [p.d()]), r=[w.d()])
            for i in range(NT):
                p = pp[n[0] % 4]
                n[0] += 1
                for kc in range(8):
                    k.op("tensor", lambda e: e.matmul(p.h[:, 0:256], lhsT=xT.h[:, kc, i * 128:(i + 1) * 128],
                                                      rhs=w.h[:, kc, 512:768], start=(kc == 0), stop=(kc == 7)),
                         r=[w.d()], w=[p.d()])
                evac(c, n[0], v.h[:, i, :], p.h[:, 0:256], [p.d()])
            k.end_phase()

        def qsel(h, a, b):
            po_ = (h % 2) * 64
            return qk.h[po_:po_ + 64, h // 2, a:b]

        def ksel(h, a, b):
            po_ = (h % 2) * 64
            return qk.h[po_:po_ + 64, 2 + h // 2, a:b]

        attention(c, "sb", qsel, ksel, 64, 0.125, v, mix, 0)


def mixer_mla(c, l, s, xT, mix):
    k, nc = c.k, c.nc
    pf = c.pfm
    with ExitStack() as st:
        w = sb(st, nc, "ml_w", [128, 8, 352], BF16)
        wsw = sb(st, nc, "ml_wsw", [128, 8, 96], BF16)
        wuq = sb(st, nc, "ml_wuq", [128, 2, 2, 384], BF16)
        wkv = sb(st, nc, "ml_wkv", [128, 2, 256], BF16)
        rope = sb(st, nc, "ml_rope", [128, 2, T], F32)
        w_in_v = c.w_in[l].rearrange("(kc p) c -> p kc c", p=128)
        k.dma("gpsimd", out=w.h[:], in_=w_in_v[:, :, 768:1120], w=[w.d()])
        k.dma("gpsimd", out=wsw.h[:], in_=c.wkr_sw[l].rearrange("(kc p) c -> p kc c", p=128), w=[wsw.d()])
        for a in range(2):
            k.dma("gpsimd", out=wuq.h[:, :, a, :], in_=c.wuq2[l, a].rearrange("(kc p) c -> p kc c", p=128), w=[wuq.d()])
        k.dma("gpsimd", out=wkv.h[:], in_=c.wukv2[l].rearrange("a p c -> p a c"), w=[wkv.d()])
        k.dma("sync", out=rope.h[:], in_=c.rope_d.rearrange("a p t -> p a t"), w=[rope.d()])
        qT = sb(st, nc, "ml_qT", [128, 4, T], BF16)
        kT = sb(st, nc, "ml_kT", [128, 4, T], BF16)
        v = sb(st, nc, "ml_v", [128, NT, 256], BF16)
        with ExitStack() as st2:
            cqg = sb(st2, nc, "ml_cqg", [128, 2, 2, 512], BF16, 2)
            sqq = sb(st2, nc, "ml_sqq", [128, 2, 2, 512], BF16, 2)
            ckvg = sb(st2, nc, "ml_ckvg", [128, 2, 512], BF16, 2)
            sqkv = sb(st2, nc, "ml_sqkv", [128, 2, 512], BF16, 2)
            rq = sb(st2, nc, "ml_rq", [128, 2, 3, 512], F32, 2)
            rkv = sb(st2, nc, "ml_rkv", [128, 2, 512], F32, 2)
            rkt = sb(st2, nc, "ml_rkt", [128, 2, 8], F32, 2)
            ta = sb(st2, nc, "ml_ta", [128, 2, 512], F32, 2)
            tb_ = sb(st2, nc, "ml_tb", [128, 2, 512], F32, 2)
            pp = [ps(st2, nc, "ml_pp%d" % i, [128, 512], F32) for i in range(8)]
            n = [0]

            def nextp():
                p = pp[n[0] % 8]
                n[0] += 1
                return p

            k.end_phase()
            for tb in range(4):
                b = tb % 2
                ts = slice(tb * 512, (tb + 1) * 512)
                specs = [(0, 128, 128), (128, 192, 64), (192, 320, 128)]
                for si, (c0, c1, M) in enumerate(specs):
                    p = nextp()
                    for kc in range(8):
                        k.op("tensor", lambda e: e.matmul(p.h[0:M, :], lhsT=w.h[:, kc, c0:c1], rhs=xT.h[:, kc, ts],
                                                          start=(kc == 0), stop=(kc == 7)), w=[p.d()])
                    if si < 2:
                        k.op("scalar", lambda e: e.activation(out=cqg.h[0:M, b, si, :], in_=p.h[0:M, :], func=AF.Identity,
                                                              scale=pf.h[0:M, QNC + si:QNC + si + 1]), r=[p.d()], w=[cqg.d(b)])
                        k.op("scalar", lambda e: e.activation(out=sqq.h[0:M, b, si, :], in_=p.h[0:M, :], func=AF.Square),
                             r=[p.d()], w=[sqq.d(b)])
                    else:
                        k.op("scalar", lambda e: e.activation(out=ckvg.h[:, b, :], in_=p.h[:], func=AF.Identity,
                                                              scale=pf.h[:, KVNC:KVNC + 1]), r=[p.d()], w=[ckvg.d(b)])
                        k.op("scalar", lambda e: e.activation(out=sqkv.h[:, b, :], in_=p.h[:], func=AF.Square),
                             r=[p.d()], w=[sqkv.d(b)])
                p = nextp()
                k.op("tensor", lambda e: e.matmul(p.h[:], lhsT=c.onesb.h[:], rhs=sqq.h[:, b, 0, :], start=True, stop=False),
                     r=[sqq.d(b)], w=[p.d()])
                k.op("tensor", lambda e: e.matmul(p.h[:], lhsT=c.onesb.h[0:64, :], rhs=sqq.h[0:64, b, 1, :], start=False, stop=True),
                     r=[sqq.d(b)], w=[p.d()])
                k.op("scalar", lambda e: e.activation(out=rq.h[:, b, 0, :], in_=p.h[:], func=AF.Ln, scale=1.0 / 192, bias=c.epsr.h[:, 0:1]),
                     r=[p.d()], w=[rq.d(b)])
                k.op("scalar", lambda e: e.activation(out=rq.h[:, b, 0, :], in_=rq.h[:, b, 0, :], func=AF.Exp, scale=-0.5),
                     r=[rq.d(b)], w=[rq.d(b)])
                for j in range(2):
                    k.op("gpsimd", lambda e: e.tensor_tensor(out=rq.h[64:96, b, 1 + j, :], in0=rq.h[64:96, b, 0, :],
                                                             in1=rope.h[64:96, j, ts], op=ALU.mult), r=[rq.d(b)], w=[rq.d(b)])
                p = nextp()
                k.op("tensor", lambda e: e.matmul(p.h[:], lhsT=c.onesb.h[:], rhs=sqkv.h[:, b, :], start=True, stop=True),
                     r=[sqkv.d(b)], w=[p.d()])
                k.op("scalar", lambda e: e.activation(out=rkv.h[:, b, :], in_=p.h[:], func=AF.Ln, scale=1.0 / 128, bias=c.epsr.h[:, 0:1]),
                     r=[p.d()], w=[rkv.d(b)])
                k.op("scalar", lambda e: e.activation(out=rkv.h[:, b, :], in_=rkv.h[:, b, :], func=AF.Exp, scale=-0.5),
                     r=[rkv.d(b)], w=[rkv.d(b)])
                p = nextp()
                for i4 in range(4):
                    k.op("tensor", lambda e: e.matmul(p.h[:, i4:i4 + 1], lhsT=sqkv.h[:, b, i4 * 128:(i4 + 1) * 128],
                                                      rhs=c.onesb.h[:, 0:1], start=True, stop=True), r=[sqkv.d(b)], w=[p.d()])
                k.op("scalar", lambda e: e.activation(out=rkt.h[:, b, 0:4], in_=p.h[:, 0:4], func=AF.Ln, scale=1.0 / 128,
                                                      bias=c.epsr.h[:, 0:1]), r=[p.d()], w=[rkt.d(b)])
                k.op("scalar", lambda e: e.activation(out=rkt.h[:, b, 4:8], in_=rkt.h[:, b, 0:4], func=AF.Exp, scale=-0.5),
                     r=[rkt.d(b)], w=[rkt.d(b)])
                for h in range(4):
                    pq = nextp()
                    pqs = nextp()
                    for a, pqq in ((0, pq), (1, pqs)):
                        k.op("tensor", lambda e: e.matmul(pqq.h[0:96, :], lhsT=wuq.h[:, 0, a, h * 96:(h + 1) * 96], rhs=cqg.h[:, b, 0, :],
                                                          start=True, stop=False), r=[cqg.d(b)], w=[pqq.d()])
                        k.op("tensor", lambda e: e.matmul(pqq.h[0:96, :], lhsT=wuq.h[0:64, 1, a, h * 96:(h + 1) * 96],
                                                          rhs=cqg.h[0:64, b, 1, :], start=False, stop=True), r=[cqg.d(b)], w=[pqq.d()])
                    k.op("vector", lambda e: e.tensor_tensor(out=qT.h[0:64, h, ts], in0=pq.h[0:64, :], in1=rq.h[0:64, b, 0, :],
                                                             op=ALU.mult), r=[pq.d(), rq.d(b)], w=[])
                    k.op("vector", lambda e: e.tensor_tensor(out=ta.h[64:96, h % 2, :], in0=pq.h[64:96, :], in1=rq.h[64:96, b, 1, :],
                                                             op=ALU.mult), r=[pq.d(), rq.d(b)], w=[ta.d(h % 2)])
                    k.op("vector", lambda e: e.tensor_tensor(out=tb_.h[64:96, h % 2, :], in0=pqs.h[64:96, :], in1=rq.h[64:96, b, 2, :],
                                                             op=ALU.mult), r=[pqs.d(), rq.d(b)], w=[tb_.d(h % 2)])
                    k.op("gpsimd", lambda e: e.tensor_tensor(out=qT.h[64:96, h, ts], in0=ta.h[64:96, h % 2, :], in1=tb_.h[64:96, h % 2, :],
                                                             op=ALU.add), r=[ta.d(h % 2), tb_.d(h % 2)], w=[])
                for h in range(4):
                    pk = nextp()
                    k.op("tensor", lambda e: e.matmul(pk.h[0:64, :], lhsT=wkv.h[:, 0, h * 64:(h + 1) * 64], rhs=ckvg.h[:, b, :],
                                                      start=True, stop=True), r=[ckvg.d(b)], w=[pk.d()])
                    k.op("vector", lambda e: e.tensor_tensor(out=kT.h[0:64, h, ts], in0=pk.h[0:64, :], in1=rkv.h[0:64, b, :],
                                                             op=ALU.mult), r=[pk.d(), rkv.d(b)], w=[])
                pkr = nextp()
                pkrs = nextp()
                for kc in range(8):
                    k.op("tensor", lambda e: e.matmul(pkr.h[0:96, :], lhsT=w.h[:, kc, 256:352], rhs=xT.h[:, kc, ts],
                                                      start=(kc == 0), stop=(kc == 7)), w=[pkr.d()])
                for kc in range(8):
                    k.op("tensor", lambda e: e.matmul(pkrs.h[0:96, :], lhsT=wsw.h[:, kc, :], rhs=xT.h[:, kc, ts],
                                                      start=(kc == 0), stop=(kc == 7)), w=[pkrs.d()])
                k.op("vector", lambda e: e.tensor_tensor(out=ta.h[64:96, 0, :], in0=pkr.h[64:96, :], in1=rope.h[64:96, 0, ts],
                                                         op=ALU.mult), r=[pkr.d()], w=[ta.d(0)])
                k.op("vector", lambda e: e.tensor_tensor(out=tb_.h[64:96, 0, :], in0=pkrs.h[64:96, :], in1=rope.h[64:96, 1, ts],
                                                         op=ALU.mult), r=[pkrs.d()], w=[tb_.d(0)])
                for h in range(4):
                    k.op("gpsimd", lambda e: e.tensor_tensor(out=kT.h[64:96, h, ts], in0=ta.h[64:96, 0, :], in1=tb_.h[64:96, 0, :],
                                                             op=ALU.add), r=[ta.d(0), tb_.d(0)], w=[])
                for i4 in range(4):
                    i = tb * 4 + i4
                    pv = nextp()
                    k.op("tensor", lambda e: e.matmul(pv.h[:, 0:256], lhsT=ckvg.h[:, b, i4 * 128:(i4 + 1) * 128], rhs=wkv.h[:, 1, :],
                                                      start=True, stop=True), r=[ckvg.d(b)], w=[pv.d()])
                    k.op("vector", lambda e: e.tensor_scalar(out=v.h[:, i, :], in0=pv.h[:, 0:256], scalar1=rkt.h[:, b, 4 + i4:5 + i4],
                                                             scalar2=None, op0=ALU.mult), r=[pv.d(), rkt.d(b)], w=[])
            k.end_phase()

        attention(c, "sm", lambda h, a, b: qT.h[0:96, h, a:b], lambda h, a, b: kT.h[0:96, h, a:b], 96, 96 ** -0.5, v, mix, 256)


def mixer_rwkv(c, l, s, xT, mix):
    k, nc = c.k, c.nc
    pf = c.pfm
    cst = c.cst
    with ExitStack() as st:
        w_in_v = c.w_in[l].rearrange("(kc p) c -> p kc c", p=128)
        wl = sb(st, nc, "rw_wl", [128, 8, 256], BF16)
        k.dma("gpsimd", out=wl.h[:], in_=w_in_v[:, :, RW0 + 1536:RW0 + 1792], w=[wl.d()])
        w2a2 = sb(st, nc, "rw_w2a2", [128, 512], BF16)
        k.dma("gpsimd", out=w2a2.h[:], in_=c.w2a2_d[l], w=[w2a2.d()])
        g2 = sb(st, nc, "rw_g2", [128, 512], BF16)
        k.dma("gpsimd", out=g2.h[:], in_=c.g2_d[l], w=[g2.d()])
        gnp = sb(st, nc, "rw_gnp", [128, 1024], F32)
        k.dma("sync", out=gnp.h[:], in_=c.pbc[l:l + 1, GNG:GNG + 1024].partition_broadcast(128), w=[gnp.d()])
        wp = sb(st, nc, "rw_wp", [128, 2, 3, 8, 128], BF16, 2)
        lw = sb(st, nc, "rw_lw", [128, T], BF16)
        sg = sb(st, nc, "rw_sg", [128, T], BF16)
        halo = sb(st, nc, "rw_halo", [128, 16], F32, 16)
        bdb = sb(st, nc, "rw_bdb", [128, 128], BF16)
        hsb = sb(st, nc, "rw_hsb", [128, 32], BF16)
        pfx = sb(st, nc, "rw_pfx", [128, 8], F32)
        mh8 = sb(st, nc, "rw_mh8", [128, 8], F32)
        k.op("vector", lambda e: e.tensor_copy(out=bdb.h[:], in_=cst.h[:, C_BD:C_BD + 128]), w=[bdb.d()])
        k.op("vector", lambda e: e.memset(hsb.h[:], 0.0), w=[hsb.d()])
        k.op("vector", lambda e: e.tensor_copy(out=hsb.h[:, 0:2], in_=cst.h[:, C_HS:C_HS + 2]), w=[hsb.d()])
        k.op("vector", lambda e: e.tensor_scalar(out=pfx.h[:, 0:4], in0=pf.h[:, W0C:W0C + 4], scalar1=-1.0, scalar2=None, op0=ALU.mult),
             w=[pfx.d()])
        k.op("vector", lambda e: e.tensor_scalar(out=pfx.h[:, 4:8], in0=pf.h[:, KAC:KAC + 4], scalar1=-1.0, scalar2=1.0,
                                                 op0=ALU.mult, op1=ALU.add), w=[pfx.d()])
        k.op("vector", lambda e: e.memset(mh8.h[:], -0.5), w=[mh8.d()])
        M4 = {}
        for nm_, col_ in (("trils", C_TRILS), ("trius", C_TRIUS), ("triui", C_TRIUI), ("bd", C_BD), ("id", C_ID)):
            M4[nm_] = sb(st, nc, "rw_m4" + nm_, [128, 4, 128], F32)
            for q_ in range(4):
                k.op("vector", lambda e: e.tensor_copy(out=M4[nm_].h[:, q_, :], in_=cst.h[:, col_:col_ + 128]), w=[M4[nm_].d()])


        raw = sb(st, nc, "rw_raw", [128, 2, 513], F32, 2)
        dtmp = sb(st, nc, "rw_dtmp", [128, 2, 512], F32, 2)
        f32names = ["R", "KX", "VX", "LWN", "CUM", "EP", "EM", "EPV", "A", "KK", "RN", "KM", "T1", "T2"]
        F = {nm: sb(st, nc, "rw_" + nm, [128, 512], F32) for nm in f32names}
        b16names = ["AT", "BT", "KT", "RT", "RK", "SQ"]
        Bb = {nm: sb(st, nc, "rw_" + nm, [128, 512], BF16) for nm in b16names}
        AT2 = sb(st, nc, "rw_AT2", [128, 2, 512], BF16)
        RT2 = sb(st, nc, "rw_RT2", [128, 2, 512], BF16)
        k.op("gpsimd", lambda e: e.memset(AT2.h[:], 0.0), w=[AT2.d()])
        k.op("gpsimd", lambda e: e.memset(RT2.h[:], 0.0), w=[RT2.d()])
        tokn = ["A_tok", "B_tok", "K_tok", "Vb_tok"]
        Tk = {nm: sb(st, nc, "rw_" + nm, [128, 4, 128], BF16) for nm in tokn}
        V_tok = sb(st, nc, "rw_V_tok", [128, 4, 128], F32)
        X = sb(st, nc, "rw_X", [128, 4, 4, 128], BF16, 4)
        Xt = sb(st, nc, "rw_Xt", [128, 4, 4, 128], BF16, 4)
        Pt = sb(st, nc, "rw_Pt", [128, 4, 4, 128], BF16, 4)
        LakT = sb(st, nc, "rw_LakT", [128, 2, 4, 128], BF16, 2)
        MrbT = sb(st, nc, "rw_MrbT", [128, 2, 4, 128], BF16, 2)
        MrkT = sb(st, nc, "rw_MrkT", [128, 2, 4, 128], BF16, 2)
        Zb = sb(st, nc, "rw_Zb", [128, 8, 64], BF16)
        U0b = sb(st, nc, "rw_U0b", [128, 8, 64], BF16)
        W1b = sb(st, nc, "rw_W1b", [128, 8, 64], BF16)
        MTb = sb(st, nc, "rw_MTb", [128, 4, 128], BF16)
        Ng = sb(st, nc, "rw_Ng", [128, 4, 128], F32, 4)
        RqT = sb(st, nc, "rw_RqT", [128, 512], BF16, 2)
        Y0 = sb(st, nc, "rw_Y0", [128, 4, 128], F32)
        yb = sb(st, nc, "rw_yb", [128, 4, 128], F32, 4)
        y2 = sb(st, nc, "rw_y2", [128, 4, 128], F32)
        bs = sb(st, nc, "rw_bs", [128, 8], F32)
        stt = sb(st, nc, "rw_stt", [128, 48], F32)
        Hb = sb(st, nc, "rw_Hb", [128, 2, 128], BF16, 2)
        pp = [ps(st, nc, "rw_pp%d" % i, [128, 512], F32) for i in range(7)]
        pdm = ps(st, nc, "rw_pdm", [128, 512], F32, 1)
        n = [0]
        ntb = [0]

        def nextp():
            p = pp[n[0] % 4]
            n[0] += 1
            return p

        rbc = [0]

        def reset():
            k.op("tensor", lambda e: e.matmul(pdm.h[:], lhsT=xT.h[:, 0, 0:128], rhs=xT.h[:, 0, 0:512], start=True, stop=True), w=[])

        def shifted_proj(wfn, j, tb, out, r=()):
            ts = slice(tb * 512, (tb + 1) * 512)
            p = nextp()
            for kc in range(8):
                k.op("tensor", lambda e: e.matmul(p.h[:], lhsT=wfn(kc), rhs=xT.h[:, kc, ts], start=(kc == 0), stop=(kc == 7)),
                     r=list(r), w=[p.d()])
            rb = rbc[0] % 2
            rbc[0] += 1
            k.op("scalar", lambda e: e.copy(out=raw.h[:, rb, 1:513], in_=p.h[:]), r=[p.d()], w=[raw.d(rb)])
            if tb == 0:
                k.op("gpsimd", lambda e: e.memset(raw.h[:, rb, 0:1], 0.0), w=[raw.d(rb)])
            else:
                k.op("gpsimd", lambda e: e.tensor_copy(out=raw.h[:, rb, 0:1], in_=halo.h[:, j:j + 1]), r=[halo.d(j)], w=[raw.d(rb)])
            k.op("gpsimd", lambda e: e.tensor_copy(out=halo.h[:, j:j + 1], in_=raw.h[:, rb, 512:513]), r=[raw.d(rb)], w=[halo.d(j)])
            k.op("vector", lambda e: e.tensor_tensor(out=dtmp.h[:, rb, :], in0=raw.h[:, rb, 0:512], in1=raw.h[:, rb, 1:513],
                                                     op=ALU.subtract), r=[raw.d(rb)], w=[dtmp.d(rb)])
            k.op("vector", lambda e: e.scalar_tensor_tensor(out=out.h[:], in0=dtmp.h[:, rb, :], scalar=pf.h[:, MU0 + j:MU0 + j + 1],
                                                            in1=raw.h[:, rb, 1:513], op0=ALU.mult, op1=ALU.add),
                 r=[dtmp.d(rb), raw.d(rb)], w=[out.d()])

        for tb in range(4):
            ts = slice(tb * 512, (tb + 1) * 512)
            shifted_proj(lambda kc: wl.h[:, kc, 0:128], 12, tb, F["T1"], r=[wl.d()])
            k.op("scalar", lambda e: e.activation(out=lw.h[0:64, ts], in_=F["T1"].h[0:64, :], func=AF.Tanh), r=[F["T1"].d()], w=[lw.d()])
            k.op("scalar", lambda e: e.copy(out=lw.h[64:128, ts], in_=F["T1"].h[64:128, :]), r=[F["T1"].d()], w=[lw.d()])
            shifted_proj(lambda kc: wl.h[:, kc, 128:256], 13, tb, F["T2"], r=[wl.d()])
            k.op("scalar", lambda e: e.activation(out=sg.h[:, ts], in_=F["T2"].h[:], func=AF.Sigmoid), r=[F["T2"].d()], w=[sg.d()])

        if c.rw_stop == 0:
            k.end_phase()
            return
        def load_wp(hp):
            sl = hp % 2
            k.dma_batch("gpsimd", [(wp.h[:, sl, a, :, :], w_in_v[:, :, RW0 + a * 512 + hp * 128:RW0 + a * 512 + (hp + 1) * 128], wp.d(sl))
                                   for a in range(3)])

        def bc4(ap):
            return ap.unsqueeze(1).to_broadcast([128, 4, 128])

        load_wp(0)
        for hp in range(4):
            if hp + 1 < 4:
                load_wp(hp + 1)
            wsl = hp % 2
            k.op("gpsimd", lambda e: e.memset(Hb.h[:, 0, :], 0.0), w=[Hb.d(0)])
            hcur = 0
            for tb in range(4):
                ts = slice(tb * 512, (tb + 1) * 512)
                shifted_proj(lambda kc: wp.h[:, wsl, 0, kc, :], hp, tb, F["R"], r=[wp.d(wsl)])
                shifted_proj(lambda kc: wp.h[:, wsl, 1, kc, :], 4 + hp, tb, F["KX"], r=[wp.d(wsl)])
                shifted_proj(lambda kc: wp.h[:, wsl, 2, kc, :], 8 + hp, tb, F["VX"], r=[wp.d(wsl)])
                p = nextp()
                k.op("tensor", lambda e: e.matmul(p.h[:], lhsT=w2a2.h[0:64, hp * 128:(hp + 1) * 128], rhs=lw.h[0:64, ts],
                                                  start=True, stop=True), r=[lw.d(), w2a2.d()], w=[p.d()])
                k.op("scalar", lambda e: e.activation(out=F["LWN"].h[:], in_=p.h[:], func=AF.Exp, scale=-1.0, bias=pfx.h[:, hp:hp + 1]),
                     r=[p.d(), pfx.d()], w=[F["LWN"].d()])
                k.op("scalar", lambda e: e.activation(out=F["LWN"].h[:], in_=F["LWN"].h[:], func=AF.Ln, bias=1.0),
                     r=[F["LWN"].d()], w=[F["LWN"].d()])
                k.op("scalar", lambda e: e.activation(out=F["LWN"].h[:], in_=F["LWN"].h[:], func=AF.Exp, scale=-1.0, bias=c.epsr.h[:, 1:2]),
                     r=[F["LWN"].d()], w=[F["LWN"].d()])
                k.op("vector", lambda e: e.tensor_tensor_scan(out=F["CUM"].h[:], data0=cst.h[:, C_RM:C_RM + 512], data1=F["LWN"].h[:],
                                                              initial=0.0, op0=ALU.mult, op1=ALU.subtract),
                     r=[F["LWN"].d()], w=[F["CUM"].d()])
                k.op("scalar", lambda e: e.activation(out=F["EP"].h[:], in_=F["CUM"].h[:], func=AF.Exp), r=[F["CUM"].d()], w=[F["EP"].d()])
                k.op("scalar", lambda e: e.activation(out=F["EM"].h[:], in_=F["CUM"].h[:], func=AF.Exp, scale=-1.0),
                     r=[F["CUM"].d()], w=[F["EM"].d()])
                k.op("gpsimd", lambda e: e.tensor_tensor(out=F["EPV"].h[:], in0=F["CUM"].h[:], in1=F["LWN"].h[:], op=ALU.add),
                     r=[F["CUM"].d(), F["LWN"].d()], w=[F["EPV"].d()])
                k.op("scalar", lambda e: e.activation(out=F["EPV"].h[:], in_=F["EPV"].h[:], func=AF.Exp), r=[F["EPV"].d()], w=[F["EPV"].d()])
                p = nextp()
                k.op("tensor", lambda e: e.matmul(p.h[:], lhsT=w2a2.h[64:128, hp * 128:(hp + 1) * 128], rhs=lw.h[64:128, ts],
                                                  start=True, stop=True), r=[lw.d(), w2a2.d()], w=[p.d()])
                k.op("scalar", lambda e: e.activation(out=F["A"].h[:], in_=p.h[:], func=AF.Sigmoid, bias=pf.h[:, A0C + hp:A0C + hp + 1]),
                     r=[p.d()], w=[F["A"].d()])
                k.op("vector", lambda e: e.tensor_scalar(out=F["KK"].h[:], in0=F["KX"].h[:], scalar1=pf.h[:, KKC + hp:KKC + hp + 1],
                                                         scalar2=None, op0=ALU.mult), r=[F["KX"].d()], w=[F["KK"].d()])
                k.op("scalar", lambda e: e.activation(out=Bb["SQ"].h[:], in_=F["KK"].h[:], func=AF.Square), r=[F["KK"].d()], w=[Bb["SQ"].d()])
                p = nextp()
                k.op("tensor", lambda e: e.matmul(p.h[:], lhsT=bdb.h[:], rhs=Bb["SQ"].h[:], start=True, stop=True),
                     r=[Bb["SQ"].d(), bdb.d()], w=[p.d()])
                k.op("vector", lambda e: e.tensor_scalar(out=F["RN"].h[:], in0=p.h[:], scalar1=1e-12, scalar2=None, op0=ALU.max),
                     r=[p.d()], w=[F["RN"].d()])
                k.op("scalar", lambda e: e.activation(out=F["RN"].h[:], in_=F["RN"].h[:], func=AF.Ln), r=[F["RN"].d()], w=[F["RN"].d()])
                k.op("scalar", lambda e: e.activation(out=F["RN"].h[:], in_=F["RN"].h[:], func=AF.Exp, scale=-0.5),
                     r=[F["RN"].d()], w=[F["RN"].d()])
                k.op("gpsimd", lambda e: e.tensor_tensor(out=F["KK"].h[:], in0=F["KK"].h[:], in1=F["RN"].h[:], op=ALU.mult),
                     r=[F["KK"].d(), F["RN"].d()], w=[F["KK"].d()])
                k.op("gpsimd", lambda e: e.tensor_scalar(out=F["KM"].h[:], in0=F["A"].h[:], scalar1=pf.h[:, KAC + hp:KAC + hp + 1],
                                                         scalar2=pfx.h[:, 4 + hp:5 + hp], op0=ALU.mult, op1=ALU.add),
                     r=[F["A"].d(), pfx.d()], w=[F["KM"].d()])
                k.op("gpsimd", lambda e: e.tensor_tensor(out=F["KM"].h[:], in0=F["KM"].h[:], in1=F["KX"].h[:], op=ALU.mult),
                     r=[F["KM"].d(), F["KX"].d()], w=[F["KM"].d()])
                k.op("gpsimd", lambda e: e.tensor_tensor(out=F["A"].h[:], in0=F["A"].h[:], in1=F["KK"].h[:], op=ALU.mult),
                     r=[F["A"].d(), F["KK"].d()], w=[F["A"].d()])
                k.op("vector", lambda e: e.scalar_tensor_tensor(out=Bb["AT"].h[:], in0=F["KK"].h[:], scalar=-1.0, in1=F["EPV"].h[:],
                                                                op0=ALU.mult, op1=ALU.mult), r=[F["KK"].d(), F["EPV"].d()], w=[Bb["AT"].d()])
                k.op("vector", lambda e: e.tensor_tensor(out=Bb["BT"].h[:], in0=F["A"].h[:], in1=F["EM"].h[:], op=ALU.mult),
                     r=[F["A"].d(), F["EM"].d()], w=[Bb["BT"].d()])
                k.op("vector", lambda e: e.tensor_tensor(out=Bb["KT"].h[:], in0=F["KM"].h[:], in1=F["EM"].h[:], op=ALU.mult),
                     r=[F["KM"].d(), F["EM"].d()], w=[Bb["KT"].d()])
                k.op("gpsimd", lambda e: e.tensor_tensor(out=Bb["RT"].h[:], in0=F["R"].h[:], in1=F["EP"].h[:], op=ALU.mult),
                     r=[F["R"].d(), F["EP"].d()], w=[Bb["RT"].d()])
                k.op("vector", lambda e: e.scalar_tensor_tensor(out=Bb["RK"].h[:], in0=F["R"].h[:], scalar=pf.h[:, RKC + hp:RKC + hp + 1],
                                                                in1=F["KM"].h[:], op0=ALU.mult, op1=ALU.mult),
                     r=[F["R"].d(), F["KM"].d()], w=[Bb["RK"].d()])
                if c.rw_stop == 1 or (hp * 4 + tb) >= int(os.environ.get("RW_NB", "99")):
                    continue
                for j_ in range(2):
                    pr_ = slice(j_ * 64, (j_ + 1) * 64)
                    k.op("scalar", lambda e: e.copy(out=AT2.h[pr_, j_, :], in_=Bb["AT"].h[pr_, :]), r=[Bb["AT"].d()], w=[AT2.d()])
                    k.op("gpsimd", lambda e: e.tensor_copy(out=RT2.h[pr_, j_, :], in_=Bb["RT"].h[pr_, :]), r=[Bb["RT"].d()], w=[RT2.d()])
                T2 = os.environ.get("T2", "abvr")
                if "x" in T2:
                    k.barrier()
                if "v" in T2:
                    if "y" in T2:
                        k.barrier()
                    p = pp[6]
                    for ci in range(4):
                        k.op("tensor", lambda e: e.matmul(p.h[:, ci * 128:(ci + 1) * 128], lhsT=F["VX"].h[:, ci * 128:(ci + 1) * 128],
                                                          rhs=c.identf.h[:], start=True, stop=True), r=[F["VX"].d()], w=[p.d()])
                    k.op("scalar", lambda e: e.copy(out=V_tok.h[:], in_=p.h[:].rearrange("p (c t) -> p c t", c=4)), r=[p.d()], w=[V_tok.d()])
                    k.op("vector", lambda e: e.tensor_copy(out=Tk["Vb_tok"].h[:], in_=p.h[:].rearrange("p (c t) -> p c t", c=4)),
                         r=[p.d()], w=[Tk["Vb_tok"].d()])
                    reset()
                alist = (("A_tok", Bb["AT"]), ("B_tok", Bb["BT"]), ("K_tok", Bb["KT"]))
                if "1" in T2:
                    alist = alist[0:1]
                if "2" in T2:
                    alist = alist[0:2]
                for nm, src_b in (alist if "a" in T2 else ()):
                    if "y" in T2:
                        k.barrier()
                    p = pp[4 + ntb[0] % 2]
                    ntb[0] += 1
                    for ci in range(4):
                        k.op("tensor", lambda e: e.matmul(p.h[:, ci * 128:(ci + 1) * 128], lhsT=src_b.h[:, ci * 128:(ci + 1) * 128],
                                                          rhs=c.identb.h[:], start=True, stop=True), r=[src_b.d()], w=[p.d()])
                    if "d" in T2:
                        k.op("vector", lambda e: e.tensor_copy(out=Tk[nm].h[:], in_=p.h[:].rearrange("p (c t) -> p c t", c=4)), r=[p.d()], w=[Tk[nm].d()])
                    else:
                        k.op("scalar", lambda e: e.copy(out=Tk[nm].h[:], in_=p.h[:].rearrange("p (c t) -> p c t", c=4)), r=[p.d()], w=[Tk[nm].d()])
                        reset()
                if "y" in T2:
                    k.barrier()
                p = nextp()
                for ci in range(4 if "b" in T2 else 0):
                    k.op("tensor", lambda e: e.matmul(p.h[:, ci * 32:(ci + 1) * 32], lhsT=Bb["RK"].h[:, ci * 128:(ci + 1) * 128], rhs=hsb.h[:],
                                                      start=True, stop=True), r=[Bb["RK"].d(), hsb.d()], w=[p.d()])
                k.op("scalar", lambda e: e.copy(out=bs.h[:].rearrange("p (c n) -> p c n", n=2),
                                                in_=p.h[:, 0:128].rearrange("p (c n) -> p c n", n=32)[:, :, 0:2]), r=[p.d()], w=[bs.d()])
                reset()

                if c.rw_stop == 2:
                    continue

                def unit(u):
                    ci, j = u // 2, u % 2
                    return ci, j, slice(j * 64, (j + 1) * 64), slice(ci * 128, (ci + 1) * 128)

                for g in range(2):
                    pL, pLT = nextp(), nextp()
                    for uu in range(4):
                        ci, j, pr, cs = unit(g * 4 + uu)
                        k.op("tensor", lambda e: e.matmul(pL.h[:, uu * 128:(uu + 1) * 128], lhsT=AT2.h[:, j, cs], rhs=Bb["BT"].h[:, cs],
                                                          start=True, stop=True), r=[AT2.d(), Bb["BT"].d()], w=[pL.d()])
                        k.op("tensor", lambda e: e.matmul(pLT.h[:, uu * 128:(uu + 1) * 128], lhsT=Bb["BT"].h[:, cs], rhs=AT2.h[:, j, cs],
                                                          start=True, stop=True), r=[AT2.d(), Bb["BT"].d()], w=[pLT.d()])
                    k.op("vector", lambda e: e.tensor_tensor(out=X.h[:, g, :, :], in0=pL.h[:].rearrange("p (c t) -> p c t", c=4),
                                                             in1=M4["trils"].h[:], op=ALU.mult), r=[pL.d()], w=[X.d(g)])
                    k.op("vector", lambda e: e.tensor_tensor(out=Xt.h[:, g, :, :], in0=pLT.h[:].rearrange("p (c t) -> p c t", c=4),
                                                             in1=M4["trius"].h[:], op=ALU.mult), r=[pLT.d()], w=[Xt.d(g)])
                    k.op("vector", lambda e: e.tensor_tensor(out=Pt.h[:, g, :, :], in0=Xt.h[:, g, :, :], in1=M4["id"].h[:],
                                                             op=ALU.add), r=[Xt.d(g)], w=[Pt.d(g)])
                    reset()
                if c.rw_stop == 3:
                    continue
                for lev in range(1, 7):
                    cur, nxt = (lev - 1) % 2, lev % 2
                    for g in range(2):
                        sc_, sn_ = cur * 2 + g, nxt * 2 + g
                        pX = nextp()
                        for uu in range(4):
                            k.op("tensor", lambda e: e.matmul(pX.h[:, uu * 128:(uu + 1) * 128], lhsT=Xt.h[:, sc_, uu, :], rhs=X.h[:, sc_, uu, :],
                                                              start=True, stop=True), r=[Xt.d(sc_), X.d(sc_)], w=[pX.d()])
                        if lev < 6:
                            pXt = nextp()
                            for uu in range(4):
                                k.op("tensor", lambda e: e.matmul(pXt.h[:, uu * 128:(uu + 1) * 128], lhsT=X.h[:, sc_, uu, :],
                                                                  rhs=Xt.h[:, sc_, uu, :], start=True, stop=True),
                                     r=[Xt.d(sc_), X.d(sc_)], w=[pXt.d()])
                        k.op("scalar", lambda e: e.copy(out=X.h[:, sn_, :, :], in_=pX.h[:].rearrange("p (c t) -> p c t", c=4)),
                             r=[pX.d()], w=[X.d(sn_)])
                        if lev < 6:
                            k.op("vector", lambda e: e.tensor_copy(out=Xt.h[:, sn_, :, :], in_=pXt.h[:].rearrange("p (c t) -> p c t", c=4)),
                                 r=[pXt.d()], w=[Xt.d(sn_)])
                        pP = nextp()
                        for uu in range(4):
                            k.op("tensor", lambda e: e.matmul(pP.h[:, uu * 128:(uu + 1) * 128], lhsT=X.h[:, sn_, uu, :], rhs=Pt.h[:, sc_, uu, :],
                                                              start=True, stop=True), r=[X.d(sn_), Pt.d(sc_)], w=[pP.d()])
                        k.op("vector", lambda e: e.tensor_tensor(out=Pt.h[:, sn_, :, :], in0=pP.h[:].rearrange("p (c t) -> p c t", c=4),
                                                                 in1=Pt.h[:, sc_, :, :], op=ALU.add), r=[pP.d(), Pt.d(sc_)], w=[Pt.d(sn_)])
                        reset()
                TTs = 0

                def TT(u):
                    return Pt.h[:, TTs * 2 + u // 4, u % 4, :]

                if c.rw_stop == 4:
                    continue
                for g in range(2):
                    pA, pB, pC = nextp(), nextp(), nextp()
                    for uu in range(4):
                        ci, j, pr, cs = unit(g * 4 + uu)
                        us = slice(uu * 128, (uu + 1) * 128)
                        k.op("tensor", lambda e: e.matmul(pA.h[:, us], lhsT=Bb["KT"].h[:, cs], rhs=AT2.h[:, j, cs], start=True, stop=True),
                             r=[Bb["KT"].d(), AT2.d()], w=[pA.d()])
                        k.op("tensor", lambda e: e.matmul(pB.h[:, us], lhsT=Bb["BT"].h[:, cs], rhs=RT2.h[:, j, cs], start=True, stop=True),
                             r=[Bb["BT"].d(), RT2.d()], w=[pB.d()])
                        k.op("tensor", lambda e: e.matmul(pC.h[:, us], lhsT=Bb["KT"].h[:, cs], rhs=RT2.h[:, j, cs], start=True, stop=True),
                             r=[Bb["KT"].d(), RT2.d()], w=[pC.d()])
                    for pq_, dst_, mk in ((pA, LakT, "trius"), (pB, MrbT, "triui"), (pC, MrkT, "triui")):
                        k.op("vector", lambda e: e.tensor_tensor(out=dst_.h[:, g, :, :], in0=pq_.h[:].rearrange("p (c t) -> p c t", c=4),
                                                                 in1=M4[mk].h[:], op=ALU.mult), r=[pq_.d()], w=[dst_.d(g)])
                        reset()
                if c.rw_stop == 5:
                    continue
                pZ = nextp()
                for u in range(8):
                    ci, j, pr, cs = unit(u)
                    k.op("tensor", lambda e: e.matmul(pZ.h[:, u * 64:(u + 1) * 64], lhsT=LakT.h[:, u // 4, u % 4, :], rhs=Tk["Vb_tok"].h[:, ci, pr],
                                                      start=True, stop=True), r=[LakT.d(u // 4), Tk["Vb_tok"].d()], w=[pZ.d()])
                k.op("scalar", lambda e: e.copy(out=Zb.h[:], in_=pZ.h[:].rearrange("p (u v) -> p u v", u=8)), r=[pZ.d()], w=[Zb.d()])
                reset()
                pU, pW = nextp(), nextp()
                for u in range(8):
                    ci, j, pr, cs = unit(u)
                    k.op("tensor", lambda e: e.matmul(pU.h[:, u * 64:(u + 1) * 64], lhsT=TT(u), rhs=Zb.h[:, u, :], start=True, stop=True),
                         r=[Pt.d(TTs * 2 + u // 4), Zb.d()], w=[pU.d()])
                    k.op("tensor", lambda e: e.matmul(pW.h[:, u * 64:(u + 1) * 64], lhsT=TT(u), rhs=Tk["A_tok"].h[:, ci, pr], start=True, stop=True),
                         r=[Pt.d(TTs * 2 + u // 4), Tk["A_tok"].d()], w=[pW.d()])
                k.op("scalar", lambda e: e.copy(out=U0b.h[:], in_=pU.h[:].rearrange("p (u v) -> p u v", u=8)), r=[pU.d()], w=[U0b.d()])
                k.op("vector", lambda e: e.tensor_copy(out=W1b.h[:], in_=pW.h[:].rearrange("p (u v) -> p u v", u=8)), r=[pW.d()], w=[W1b.d()])
                reset()
                if c.rw_stop == 6:
                    continue
                pM, pN = nextp(), nextp()
                for ci in range(4):
                    cs = slice(ci * 128, (ci + 1) * 128)
                    w1p = W1b.h[:, ci * 2:ci * 2 + 2, :].rearrange("p a v -> p (a v)")
                    u0p = U0b.h[:, ci * 2:ci * 2 + 2, :].rearrange("p a v -> p (a v)")
                    k.op("tensor", lambda e: e.matmul(pM.h[:, cs], lhsT=w1p, rhs=Tk["B_tok"].h[:, ci, :], start=True, stop=False),
                         r=[W1b.d(), Tk["B_tok"].d()], w=[pM.d()])
                    k.op("tensor", lambda e: e.matmul(pM.h[:, cs], lhsT=c.identb.h[:], rhs=c.identb.h[:], start=False, stop=True), w=[pM.d()])
                    k.op("tensor", lambda e: e.matmul(pN.h[:, cs], lhsT=Tk["B_tok"].h[:, ci, :], rhs=u0p, start=True, stop=False),
                         r=[U0b.d(), Tk["B_tok"].d()], w=[pN.d()])
                    k.op("tensor", lambda e: e.matmul(pN.h[:, cs], lhsT=Tk["K_tok"].h[:, ci, :], rhs=Tk["Vb_tok"].h[:, ci, :], start=False, stop=True),
                         r=[Tk["K_tok"].d(), Tk["Vb_tok"].d()], w=[pN.d()])
                k.op("vector", lambda e: e.tensor_tensor(out=MTb.h[:], in0=pM.h[:].rearrange("p (c t) -> p c t", c=4),
                                                         in1=M4["bd"].h[:], op=ALU.mult), r=[pM.d()], w=[MTb.d()])
                reset()
                for ci in range(4):
                    cs = slice(ci * 128, (ci + 1) * 128)
                    k.op("vector", lambda e: e.scalar_tensor_tensor(out=Ng.h[:, ci, :], in0=pN.h[:, cs], scalar=F["EP"].h[:, ci * 128 + 127:ci * 128 + 128],
                                                                    in1=cst.h[:, C_BD:C_BD + 128], op0=ALU.mult, op1=ALU.mult),
                         r=[pN.d(), F["EP"].d()], w=[Ng.d(ci)])
                if c.rw_stop == 7:
                    continue
                for j in range(2):
                    pR = nextp()
                    pr = slice(j * 64, (j + 1) * 64)
                    for ci in range(4):
                        u = ci * 2 + j
                        w1p = W1b.h[:, ci * 2:ci * 2 + 2, :].rearrange("p a v -> p (a v)")
                        k.op("tensor", lambda e: e.matmul(pR.h[:, ci * 128:(ci + 1) * 128], lhsT=w1p, rhs=MrbT.h[:, u // 4, u % 4, :],
                                                          start=True, stop=True), r=[W1b.d(), MrbT.d(u // 4)], w=[pR.d()])
                    k.op("vector", lambda e: e.tensor_tensor(out=RqT.h[pr, :], in0=pR.h[pr, :], in1=Bb["RT"].h[pr, :], op=ALU.add),
                         r=[pR.d(), Bb["RT"].d()], w=[RqT.d(j)])
                    reset()
                pY0 = nextp()
                for u in range(8):
                    ci, j, pr, cs = unit(u)
                    k.op("tensor", lambda e: e.matmul(pY0.h[:, u * 64:(u + 1) * 64], lhsT=MrbT.h[:, u // 4, u % 4, :], rhs=U0b.h[:, u, :],
                                                      start=True, stop=False), r=[MrbT.d(u // 4), U0b.d()], w=[pY0.d()])
                    k.op("tensor", lambda e: e.matmul(pY0.h[:, u * 64:(u + 1) * 64], lhsT=MrkT.h[:, u // 4, u % 4, :], rhs=Tk["Vb_tok"].h[:, ci, pr],
                                                      start=False, stop=True), r=[MrkT.d(u // 4), Tk["Vb_tok"].d()], w=[pY0.d()])
                k.op("scalar", lambda e: e.copy(out=Y0.h[:], in_=pY0.h[:].rearrange("p (c t) -> p c t", c=4)), r=[pY0.d()], w=[Y0.d()])
                reset()
                if c.rw_stop == 8:
                    continue
                for ci in range(4):
                    cs = slice(ci * 128, (ci + 1) * 128)
                    pY, pH = nextp(), nextp()
                    k.op("tensor", lambda e: e.matmul(pY.h[:, 0:128], lhsT=RqT.h[:, cs], rhs=Hb.h[:, hcur, :], start=True, stop=True),
                         r=[RqT.d(0), RqT.d(1), Hb.d(hcur)], w=[pY.d()])
                    k.op("tensor", lambda e: e.matmul(pH.h[:, 0:128], lhsT=MTb.h[:, ci, :], rhs=Hb.h[:, hcur, :], start=True, stop=True),
                         r=[MTb.d(), Hb.d(hcur)], w=[pH.d()])
                    k.op("vector", lambda e: e.scalar_tensor_tensor(out=Hb.h[:, 1 - hcur, :], in0=pH.h[:, 0:128],
                                                                    scalar=F["EP"].h[:, ci * 128 + 127:ci * 128 + 128], in1=Ng.h[:, ci, :],
                                                                    op0=ALU.mult, op1=ALU.add),
                         r=[pH.d(), Ng.d(ci), F["EP"].d()], w=[Hb.d(1 - hcur)])
                    k.op("vector", lambda e: e.tensor_tensor(out=yb.h[:, ci, :], in0=pY.h[:, 0:128], in1=Y0.h[:, ci, :], op=ALU.add),
                         r=[pY.d(), Y0.d()], w=[yb.d(ci)])
                    reset()
                    hcur = 1 - hcur
                if c.rw_stop == 9:
                    continue
                ybd = yb.all()
                y8 = yb.h[:].rearrange("p c (j v) -> p (c j) v", j=2)
                y28 = y2.h[:].rearrange("p c (j v) -> p (c j) v", j=2)
                v8 = V_tok.h[:].rearrange("p c (j v) -> p (c j) v", j=2)

                def b64(ap):
                    return ap.unsqueeze(2).to_broadcast([128, 8, 64])

                k.op("vector", lambda e: e.tensor_reduce(out=stt.h[:, 0:8], in_=y8, axis=AX.X, op=ALU.add), r=ybd, w=[stt.d()])
                k.op("scalar", lambda e: e.activation(out=y2.h[:], in_=yb.h[:], func=AF.Square), r=ybd, w=[y2.d()])
                k.op("vector", lambda e: e.tensor_reduce(out=stt.h[:, 8:16], in_=y28, axis=AX.X, op=ALU.add), r=[y2.d()], w=[stt.d()])
                k.op("vector", lambda e: e.tensor_scalar(out=stt.h[:, 16:24], in0=stt.h[:, 0:8], scalar1=1.0 / 64, scalar2=None, op0=ALU.mult),
                     r=[stt.d()], w=[stt.d()])
                k.op("vector", lambda e: e.tensor_tensor(out=stt.h[:, 24:32], in0=stt.h[:, 16:24], in1=stt.h[:, 16:24], op=ALU.mult),
                     r=[stt.d()], w=[stt.d()])
                k.op("vector", lambda e: e.scalar_tensor_tensor(out=stt.h[:, 32:40], in0=stt.h[:, 8:16], scalar=1.0 / 64, in1=stt.h[:, 24:32],
                                                                op0=ALU.mult, op1=ALU.subtract), r=[stt.d()], w=[stt.d()])
                k.op("gpsimd", lambda e: e.tensor_scalar(out=stt.h[:, 32:40], in0=stt.h[:, 32:40], scalar1=GN_EPS, scalar2=None, op0=ALU.add),
                     r=[stt.d()], w=[stt.d()])
                k.op("gpsimd", lambda e: e.tensor_tensor(out=stt.h[:, 40:48], in0=stt.h[:, 32:40], in1=mh8.h[:], op=ALU.pow),
                     r=[stt.d(), mh8.d()], w=[stt.d()])
                gcol = slice(hp * 128, (hp + 1) * 128)
                for u in range(8):
                    ci, j = u // 2, u % 2
                    ysl = yb.h[:, ci, j * 64:(j + 1) * 64]
                    k.op("vector", lambda e: e.tensor_scalar(out=ysl, in0=ysl, scalar1=stt.h[:, 16 + u:17 + u], scalar2=stt.h[:, 40 + u:41 + u],
                                                             op0=ALU.subtract, op1=ALU.mult), r=[stt.d()] + ybd, w=ybd)
                for ci in range(4):
                    k.op("gpsimd", lambda e: e.tensor_tensor(out=yb.h[:, ci, :], in0=yb.h[:, ci, :], in1=gnp.h[:, gcol], op=ALU.mult),
                         r=[gnp.d()] + ybd, w=ybd)
                    k.op("gpsimd", lambda e: e.tensor_tensor(out=yb.h[:, ci, :], in0=yb.h[:, ci, :],
                                                             in1=gnp.h[:, 512 + hp * 128:512 + (hp + 1) * 128], op=ALU.add),
                         r=[gnp.d()] + ybd, w=ybd)
                for u in range(8):
                    ci, j = u // 2, u % 2
                    ysl = yb.h[:, ci, j * 64:(j + 1) * 64]
                    k.op("vector", lambda e: e.scalar_tensor_tensor(out=ysl, in0=V_tok.h[:, ci, j * 64:(j + 1) * 64], scalar=bs.h[:, u:u + 1],
                                                                    in1=ysl, op0=ALU.mult, op1=ALU.add), r=[V_tok.d(), bs.d()] + ybd, w=ybd)
                pG = nextp()
                for ci in range(4):
                    t0 = tb * 512 + ci * 128
                    k.op("tensor", lambda e: e.matmul(pG.h[:, ci * 128:(ci + 1) * 128], lhsT=sg.h[:, t0:t0 + 128], rhs=g2.h[:, gcol],
                                                      start=True, stop=True), r=[sg.d(), g2.d()], w=[pG.d()])
                mc = 512 + hp * 128
                k.op("vector", lambda e: e.tensor_tensor(out=mix.h[:, tb * 4:(tb + 1) * 4, mc:mc + 128], in0=yb.h[:],
                                                         in1=pG.h[:].rearrange("p (c t) -> p c t", c=4), op=ALU.mult), r=[pG.d()] + ybd, w=[])
        k.end_phase()


def phase_A(c, l, s, src, mix, mixers):
    k, nc = c.k, c.nc
    if not mixers:
        return
    with ExitStack() as st:
        xT = sb(st, nc, "xT", [128, 8, T], BF16)
        phase_xT(c, src, s, xT)
        if "sb" in mixers:
            mixer_sb(c, l, s, xT, mix)
        if "mla" in mixers:
            mixer_mla(c, l, s, xT, mix)
        if "rwkv" in mixers:
            mixer_rwkv(c, l, s, xT, mix)


def build(nl, mixers=("sb", "mla", "rwkv"), dbg_mix=False):
    nc = bass.Bass("TRN2", target_bir_lowering=False)
    c = Ctx()
    c.nc = nc
    c.k = k = K(nc)
    import os
    c.rw_stop = int(os.environ.get('RW_STOP', '99'))
    dt = nc.dram_tensor
    x_in = dt("x", [NSEQ * T, D], F32, kind="ExternalInput").ap()
    y_out = dt("y", [NSEQ * T, D], F32, kind="ExternalOutput").ap()
    c.w_in = dt("w_in", [nl, D, IN_COLS], F32, kind="ExternalInput").ap()
    c.w_o = dt("w_o", [nl, D, D], F32, kind="ExternalInput").ap()
    c.w_up = dt("w_up", [nl, D, 2 * DFF], F32, kind="ExternalInput").ap()
    c.w_down = dt("w_down", [nl, DFF, D], F32, kind="ExternalInput").ap()
    c.pfm_d = dt("pfm", [nl, 128, NPF], F32, kind="ExternalInput").ap()
    c.pbc = dt("pbc", [nl, NPB], F32, kind="ExternalInput").ap()
    c.cst_d = dt("cst", [128, NCST], F32, kind="ExternalInput").ap()
    c.wkr_sw = dt("wkr_sw", [nl, D, 96], F32, kind="ExternalInput").ap()
    c.wuq2 = dt("wuq2", [nl, 2, 256, 384], F32, kind="ExternalInput").ap()
    c.wukv2 = dt("wukv2", [nl, 2, 128, 256], F32, kind="ExternalInput").ap()
    c.rope_d = dt("rope", [2, 128, T], F32, kind="ExternalInput").ap()
    c.w2a2_d = dt("w2a2", [nl, 128, 512], F32, kind="ExternalInput").ap()
    c.g2_d = dt("g2", [nl, 128, 512], F32, kind="ExternalInput").ap()
    c.x1d = dt("x1d", [NSEQ * T, D], F32).ap()
    c.x1d_dep = Dep(True)
    scr = [dt("xs0", [NSEQ * T, D], F32).ap(), dt("xs1", [NSEQ * T, D], F32).ap()]
    scr_dep = [Dep(True), Dep(True)]
    ydep = Dep(True)
    if dbg_mix:
        c.mix_d = dt("mixd", [NSEQ * T, D], BF16, kind="ExternalOutput").ap()
        c.mix_dep = Dep(True)

    asb = nc.alloc_sbuf_tensor
    c.cst = Buf(asb("cst_sb", [128, NCST], F32), 1, True)
    c.identf = Buf(c.cst.h[:, C_ID:C_ID + 128])
    c.identb = Buf(asb("identb", [128, 128], BF16))
    c.mhalf = Buf(asb("mhalf", [128, 1], F32))
    c.pfm = Buf(asb("pfm_sb", [128, NPF], F32), 1, True)

    k.dma("sync", out=c.cst.h[:], in_=c.cst_d, w=[c.cst.d()])
    k.op("vector", lambda e: e.tensor_copy(out=c.identb.h[:], in_=c.cst.h[:, C_ID:C_ID + 128]), r=[c.cst.d()], w=[c.identb.d()])
    k.op("vector", lambda e: e.memset(c.mhalf.h[:], -0.5), w=[c.mhalf.d()])
    c.onesb = Buf(asb("onesb", [128, 128], BF16))
    k.op("vector", lambda e: e.memset(c.onesb.h[:], 1.0), w=[c.onesb.d()])
    c.epsr = Buf(asb("epsr", [128, 2], F32))
    k.op("vector", lambda e: e.memset(c.epsr.h[:, 0:1], RMS_EPS), w=[c.epsr.d()])
    k.op("vector", lambda e: e.memset(c.epsr.h[:, 1:2], -0.5), w=[c.epsr.d()])
    k.end_phase()

    for l in range(nl):
        src = x_in if l == 0 else scr[(l - 1) % 2]
        dst, dst_dep = (y_out, ydep) if l == nl - 1 else (scr[l % 2], scr_dep[l % 2])
        k.dma("sync", out=c.pfm.h[:], in_=c.pfm_d[l], w=[c.pfm.d()])
        k.end_phase()
        for s in range(NSEQ):
            st_mix = ExitStack()
            mix = Buf(st_mix.enter_context(nc.sbuf_tensor(_uname("mix"), [128, NT, D], BF16, side="right")))
            k.op("gpsimd", lambda e: e.memset(mix.h[:], 0.0))
            k.end_phase()
            phase_A(c, l, s, src, mix, mixers)
            if dbg_mix and l == nl - 1:
                for i in range(NT):
                    k.dma("sync", out=c.mix_d[s * T + i * 128:s * T + (i + 1) * 128, :], in_=mix.h[:, i, :],
                          w=[c.mix_dep], nowaw=True)
                k.end_phase()
            with ExitStack() as st_x1:
                x1T = sb(st_x1, nc, "x1T", [128, 8, T], BF16)
                phase_B(c, l, s, src, mix, x1T)
                st_mix.close()
                phase_C(c, l, s, dst, dst_dep, x1T)
    k.end_phase()
    return nc


def make_consts():
    cst = np.zeros((128, NCST), np.float32)
    p = np.arange(128)
    cst[:, C_ID:C_ID + 128] = np.eye(128, dtype=np.float32)
    cst[:, C_TRILS:C_TRILS + 128] = (p[None, :] < p[:, None])
    cst[:, C_TRIUS:C_TRIUS + 128] = (p[:, None] < p[None, :])
    cst[:, C_TRILI:C_TRILI + 128] = (p[None, :] <= p[:, None])
    cst[:, C_TRIUI:C_TRIUI + 128] = (p[:, None] <= p[None, :])
    cst[:, C_BD:C_BD + 128] = ((p[:, None] // 64) == (p[None, :] // 64))
    cst[:, C_HS] = (p // 64 == 0)
    cst[:, C_HS + 1] = (p // 64 == 1)
    cst[:, C_RM:C_RM + 512] = (np.arange(512) % 128 != 0)[None, :]
    return cst


def make_rope():
    inv_freq = (1.0 / (np.float32(10000.0) ** (np.arange(0, 32, 2, dtype=np.float32) / np.float32(32)))).astype(np.float32)
    ang = np.arange(T, dtype=np.float32)[:, None] * inv_freq[None, :]
    cos, sin = np.cos(ang).astype(np.float32), np.sin(ang).astype(np.float32)
    r = np.zeros((2, 128, T), np.float32)
    r[0, 64:80] = cos.T
    r[0, 80:96] = cos.T
    r[1, 64:80] = -sin.T
    r[1, 80:96] = sin.T
    return r


def colvec(v, n):
    return np.ascontiguousarray(v.reshape(n, 128).T)


def pack_params(inp, l):
    pfm = np.zeros((128, NPF), np.float32)
    pfm[:, MU0:MU0 + 14] = colvec(inp["rwkv_mu"][l], 14)
    pfm[:, W0C:W0C + 4] = colvec(inp["rwkv_w0"][l], 4)
    pfm[:, A0C:A0C + 4] = colvec(inp["rwkv_a0"][l], 4)
    pfm[:, KKC:KKC + 4] = colvec(inp["rwkv_k_k"][l], 4)
    pfm[:, KAC:KAC + 4] = colvec(inp["rwkv_k_a"][l], 4)
    pfm[:, RKC:RKC + 4] = colvec(inp["rwkv_r_k"][l].reshape(-1), 4)
    qn = np.zeros(256, np.float32)
    qn[:192] = inp["mla_q_norm"][l]
    pfm[:, QNC:QNC + 2] = colvec(qn, 2)
    pfm[:, KVNC:KVNC + 1] = colvec(inp["mla_kv_norm"][l], 1)
    for j in range(3):
        pfm[:, CWC + j * NFC:CWC + (j + 1) * NFC] = colvec(inp["ffn_conv_w"][l, j], NFC)
    pfm[:, CBC:CBC + NFC] = colvec(inp["ffn_conv_b"][l], NFC)
    pbc = np.concatenate([inp["ln1_g"][l], inp["ln1_b"][l], inp["ln2_g"][l], inp["ln2_b"][l],
                          inp["rwkv_gn_g"][l], inp["rwkv_gn_b"][l]]).astype(np.float32)
    return pfm, pbc


_PROG = {}


def get_prog(nl, **kw):
    key = (nl, tuple(sorted(kw.items())))
    if key not in _PROG:
        _PROG[key] = build(nl, **kw)
    return _PROG[key]


def host_inputs(inp, layers):
    f = lambda a: np.ascontiguousarray(np.asarray(a, dtype=np.float32))
    packed = [pack_params(inp, l) for l in layers]
    shared = {
        "w_in": f(inp["w_in"][layers]),
        "w_o": f(inp["w_o"][layers]),
        "w_up": f(inp["ffn_w_up"][layers]),
        "w_down": f(inp["ffn_w_down"][layers]),
        "pfm": f(np.stack([p[0] for p in packed])),
        "pbc": f(np.stack([p[1] for p in packed])),
        "cst": make_consts(),
        "rope": make_rope(),
    }
    perm = np.concatenate([np.arange(16, 32), np.arange(0, 16)])
    wkr, wuq2, wukv2 = [], [], []
    for l in layers:
        w_in = np.asarray(inp["w_in"][l], np.float32)
        wkr.append(np.concatenate([w_in[:, 1024:1088], w_in[:, 1088:1120][:, perm]], axis=1))
        wq = np.asarray(inp["mla_w_uq"][l], np.float32)
        wqs = wq.copy().reshape(192, 4, 96)
        wqs[:, :, 64:96] = wqs[:, :, 64:96][:, :, perm]
        z = np.zeros((2, 256, 384), np.float32)
        z[0, :192] = wq
        z[1, :192] = wqs.reshape(192, 384)
        wuq2.append(z)
        wkv = np.asarray(inp["mla_w_ukv"][l], np.float32).reshape(128, 4, 128)
        wukv2.append(np.stack([wkv[:, :, 0:64].reshape(128, 256), wkv[:, :, 64:128].reshape(128, 256)]))
    shared["w2a2"] = f(np.stack([np.concatenate([inp["rwkv_w2"][l], inp["rwkv_a2"][l]], axis=0) for l in layers]))
    shared["g2"] = f(np.stack([inp["rwkv_g2"][l] for l in layers]))
    shared["wkr_sw"] = f(np.stack(wkr))
    shared["wuq2"] = f(np.stack(wuq2))
    shared["wukv2"] = f(np.stack(wukv2))
    return shared


def kernel(**inputs):
    inp = {kk: np.asarray(v) for kk, v in inputs.items()}
    x = np.ascontiguousarray(inp["x"], dtype=np.float32)
    B = x.shape[0]
    per = B // NCORES
    nc = get_prog(DEPTH)
    shared = host_inputs(inp, list(range(DEPTH)))
    in_maps = []
    for ci in range(NCORES):
        m = dict(shared)
        m["x"] = np.ascontiguousarray(x[ci * per:(ci + 1) * per].reshape(per * T, D))
        in_maps.append(m)
    res = run_bass_kernel_spmd(nc, in_maps, core_ids=list(range(NCORES)))
    out = np.concatenate([r["y"].reshape(per, T, D) for r in res.results], axis=0)
    return out.astype(np.float32)
```

```python
import math
import os
from contextlib import ExitStack

import numpy as np
import concourse.bass as bass
import concourse.mybir as mybir
from concourse.bass_utils import run_bass_kernel_spmd

F32 = mybir.dt.float32
BF16 = mybir.dt.bfloat16
AF = mybir.ActivationFunctionType
ALU = mybir.AluOpType
AX = mybir.AxisListType

NCORES = 8
DEPTH = 4
T = 2048
NT = 16
D = 1024
NSEQ = 2
IN_COLS = 2912
DFF = 2816
NFC = 22
ALPHA = (2 * DEPTH) ** 0.25
LN_EPS = 1e-5
RMS_EPS = 1e-6
GN_EPS = 64e-5
RW0 = 1120

MU0, W0C, A0C, KKC, KAC, RKC, QNC, KVNC, CWC, CBC, NPF = 0, 14, 18, 22, 26, 30, 34, 36, 37, 103, 125
LN1G, LN1B, LN2G, LN2B, GNG, GNB, NPB = 0, 1024, 2048, 3072, 4096, 4608, 5120
C_ID, C_TRILS, C_TRIUS, C_TRILI, C_TRIUI, C_BD, C_HS, C_RM, NCST = 0, 128, 256, 384, 512, 640, 768, 770, 1282

SAME_ENG_SYNC = bool(int(os.environ.get("SES", "1")))


class Sem:
    __slots__ = ("h", "cnt", "q")

    def __init__(self, h, q):
        self.h = h
        self.cnt = 0
        self.q = q


class Dep:
    __slots__ = ("w", "r", "sem", "persist")

    def __init__(self, persist=False):
        self.w = None
        self.r = {}
        self.sem = None
        self.persist = persist


class Buf:
    def __init__(self, h, nslots=1, persist=False):
        self.h = h
        self.deps = [Dep(persist) for _ in range(nslots)]

    def d(self, i=0):
        return self.deps[i]

    def all(self):
        return list(self.deps)


class Eng:
    def __init__(self, name, e, sem):
        self.name = name
        self.e = e
        self.sem = sem
        self.cnt = 0
        self.seen = {}


class K:
    def __init__(self, nc):
        self.nc = nc
        self.eng = {}
        for n in ["tensor", "vector", "scalar", "gpsimd", "sync"]:
            self.eng[n] = Eng(n, getattr(nc, n), nc.alloc_semaphore("e_" + n))
        self.all_sems = []
        self.free_sems = {"sync": [], "gpsimd": []}
        self.phase_deps = []

    def _wait(self, es, toks):
        need = {}
        for t in toks:
            if t is None:
                continue
            kk = id(t[0])
            if kk not in need or need[kk][1] < t[1]:
                need[kk] = t
        for kk, (sem, val) in need.items():
            if es.seen.get(kk, 0) >= val:
                continue
            if sem is es.sem:
                if es.name == "tensor" or not SAME_ENG_SYNC:
                    continue
            es.e.wait_ge(sem, val)
            es.seen[kk] = val

    @staticmethod
    def _toks(r, w):
        toks = [d.w for d in r]
        for d in w:
            toks.append(d.w)
            toks.extend(d.r.values())
        return toks

    def op(self, en, fn, r=(), w=()):
        es = self.eng[en]
        self._wait(es, self._toks(r, w))
        ins = fn(es.e)
        es.cnt += 1
        ins.then_inc(es.sem, 1)
        tok = (es.sem, es.cnt)
        for d in r:
            d.r[id(es.sem)] = tok
        for d in w:
            d.w = tok
            d.r = {}
        return tok

    def _dsem(self, d0, qn):
        if d0.sem is not None and d0.sem.q != qn:
            d0.sem = None
        if d0.sem is None:
            if self.free_sems[qn]:
                d0.sem = self.free_sems[qn].pop()
            else:
                d0.sem = Sem(self.nc.alloc_semaphore("d%d" % len(self.all_sems)), qn)
                self.all_sems.append(d0.sem)
            if not d0.persist:
                self.phase_deps.append(d0)
        return d0.sem

    def dma(self, qn, out, in_, r=(), w=(), nowaw=False, semdep=None):
        es = self.eng[qn]
        if nowaw:
            self._wait(es, [d.w for d in r])
        else:
            self._wait(es, self._toks(r, w))
        sm = self._dsem(semdep if semdep is not None else w[0], qn)
        sm.cnt += 16
        es.e.dma_start(out=out, in_=in_).then_inc(sm.h, 16)
        tok = (sm.h, sm.cnt)
        for d in r:
            d.r[id(sm.h)] = tok
        for d in w:
            d.w = tok
            if not nowaw:
                d.r = {}
        return tok

    def dma_batch(self, qn, items):
        es = self.eng[qn]
        deps = [it[2] for it in items]
        self._wait(es, self._toks((), deps))
        sm = self._dsem(deps[0], qn)
        for out, in_, d in items:
            sm.cnt += 16
            es.e.dma_start(out=out, in_=in_).then_inc(sm.h, 16)
        tok = (sm.h, sm.cnt)
        for d in deps:
            d.w = tok
            d.r = {}

    def barrier(self):
        toks = [(es.sem, es.cnt) for es in self.eng.values() if es.cnt > 0]
        toks += [(sm.h, sm.cnt) for sm in self.all_sems if sm.cnt > 0]
        for es in self.eng.values():
            self._wait(es, toks)

    def end_phase(self):
        self.barrier()
        for d in self.phase_deps:
            if d.sem is not None:
                self.free_sems[d.sem.q].append(d.sem)
                d.sem = None
        self.phase_deps = []


class Ctx:
    pass


_UID = [0]


def _uname(name):
    _UID[0] += 1
    return "%s_%d" % (name, _UID[0])


def sb(st, nc, name, shape, dt, nslots=1):
    return Buf(st.enter_context(nc.sbuf_tensor(_uname(name), shape, dt)), nslots)


def ps(st, nc, name, shape, dt=F32, nslots=1):
    return Buf(st.enter_context(nc.psum_tensor(_uname(name), shape, dt)), nslots)


def layernorm(c, t_ap, t_deps, out_ap, out_deps, g_ap, b_ap, sc, b, gdeps=()):
    k = c.k
    stats, mv, rs = sc["stats"], sc["mv"], sc["rs"]
    for hf in range(2):
        k.op("vector", lambda e: e.bn_stats(out=stats.h[:, b, hf * 6:(hf + 1) * 6],
                                            in_=t_ap[:, hf * 512:(hf + 1) * 512]),
             r=t_deps, w=[stats.d(b)])
    k.op("vector", lambda e: e.bn_aggr(out=mv.h[:, b, :], in_=stats.h[:, b, :]),
         r=[stats.d(b)], w=[mv.d(b)])
    k.op("gpsimd", lambda e: e.tensor_scalar(out=rs.h[:, b, 0:1], in0=mv.h[:, b, 1:2], scalar1=LN_EPS,
                                             scalar2=None, op0=ALU.add),
         r=[mv.d(b)], w=[rs.d(b)])
    k.op("gpsimd", lambda e: e.tensor_tensor(out=rs.h[:, b, 1:2], in0=rs.h[:, b, 0:1],
                                             in1=c.mhalf.h[:, 0:1], op=ALU.pow),
         r=[rs.d(b)], w=[rs.d(b)])
    k.op("vector", lambda e: e.tensor_scalar(out=t_ap, in0=t_ap, scalar1=mv.h[:, b, 0:1],
                                             scalar2=rs.h[:, b, 1:2], op0=ALU.subtract, op1=ALU.mult),
         r=[mv.d(b), rs.d(b)] + t_deps, w=t_deps)
    k.op("gpsimd", lambda e: e.tensor_tensor(out=t_ap, in0=t_ap, in1=g_ap, op=ALU.mult),
         r=list(t_deps) + list(gdeps), w=t_deps)
    k.op("vector", lambda e: e.tensor_tensor(out=out_ap, in0=t_ap, in1=b_ap, op=ALU.add),
         r=list(t_deps) + list(gdeps), w=out_deps)


def phase_xT(c, src, s, xT):
    k, nc = c.k, c.nc
    with ExitStack() as st:
        xt = sb(st, nc, "p1_xt", [128, 3, D], F32, 3)
        pt = [ps(st, nc, "p1_ps%d" % i, [128, 8, 128], F32) for i in range(2)]
        for i in range(NT):
            sl = i % 3
            b = i % 2
            k.dma("sync", out=xt.h[:, sl, :], in_=src[s * T + i * 128:s * T + (i + 1) * 128, :], w=[xt.d(sl)])
            for kc in range(8):
                k.op("tensor", lambda e: e.matmul(pt[b].h[:, kc, :], lhsT=xt.h[:, sl, kc * 128:(kc + 1) * 128],
                                                  rhs=c.identf.h[:], is_transpose=True),
                     r=[xt.d(sl)], w=[pt[b].d()])
            en = "scalar" if i % 2 else "vector"
            if en == "scalar":
                k.op("scalar", lambda e: e.copy(out=xT.h[:, :, i * 128:(i + 1) * 128], in_=pt[b].h[:]),
                     r=[pt[b].d()], w=[])
            else:
                k.op("vector", lambda e: e.tensor_copy(out=xT.h[:, :, i * 128:(i + 1) * 128], in_=pt[b].h[:]),
                     r=[pt[b].d()], w=[])
        k.end_phase()


def phase_B(c, l, s, src, mix, x1T):
    k, nc = c.k, c.nc
    with ExitStack() as st:
        wo = sb(st, nc, "pb_wo", [128, 8, D], BF16, 8)
        k.dma_batch("gpsimd", [(wo.h[:, kc, :], c.w_o[l, kc * 128:(kc + 1) * 128, :], wo.d(kc)) for kc in range(8)])
        lnp = sb(st, nc, "pb_lnp", [128, 2048], F32)
        k.dma("sync", out=lnp.h[:], in_=c.pbc[l:l + 1, LN1G:LN1G + 2048].partition_broadcast(128), w=[lnp.d()])
        xt = sb(st, nc, "pb_xt", [128, 2, D], F32, 2)
        tt = sb(st, nc, "pb_tt", [128, 2, D], F32, 2)
        xo = sb(st, nc, "pb_xo", [128, 2, D], F32, 2)
        mixT = sb(st, nc, "pb_mixT", [128, 2, 8, 128], BF16, 2)
        sc = dict(stats=sb(st, nc, "pb_stats", [128, 2, 12], F32, 2), mv=sb(st, nc, "pb_mv", [128, 2, 2], F32, 2),
                  rs=sb(st, nc, "pb_rs", [128, 2, 2], F32, 2))
        pT = [ps(st, nc, "pb_pT%d" % i, [128, 8, 128], BF16) for i in range(2)]
        pso = [ps(st, nc, "pb_pso%d" % i, [128, D], F32, 2) for i in range(2)]
        px = ps(st, nc, "pb_px", [128, 8, 128], F32)
        for i in range(NT):
            b = i % 2
            r0 = s * T + i * 128
            k.dma("sync", out=xt.h[:, b, :], in_=src[r0:r0 + 128, :], w=[xt.d(b)])
            for kc in range(8):
                k.op("tensor", lambda e: e.matmul(pT[b].h[:, kc, :], lhsT=mix.h[:, i, kc * 128:(kc + 1) * 128],
                                                  rhs=c.identb.h[:], is_transpose=True), w=[pT[b].d()])
            k.op("scalar", lambda e: e.copy(out=mixT.h[:, b, :, :], in_=pT[b].h[:]), r=[pT[b].d()], w=[mixT.d(b)])
            for hf in range(2):
                for kc in range(8):
                    k.op("tensor", lambda e: e.matmul(pso[b].h[:, hf * 512:(hf + 1) * 512], lhsT=mixT.h[:, b, kc, :],
                                                      rhs=wo.h[:, kc, hf * 512:(hf + 1) * 512],
                                                      start=(kc == 0), stop=(kc == 7)),
                         r=[mixT.d(b), wo.d(kc)], w=[pso[b].d(hf)])
            for hf in range(2):
                k.op("vector", lambda e: e.scalar_tensor_tensor(
                    out=tt.h[:, b, hf * 512:(hf + 1) * 512], in0=xt.h[:, b, hf * 512:(hf + 1) * 512], scalar=ALPHA,
                    in1=pso[b].h[:, hf * 512:(hf + 1) * 512], op0=ALU.mult, op1=ALU.add),
                    r=[xt.d(b), pso[b].d(hf)], w=[tt.d(b)])
            layernorm(c, tt.h[:, b, :], [tt.d(b)], xo.h[:, b, :], [xo.d(b)], lnp.h[:, 0:1024], lnp.h[:, 1024:2048], sc, b, [lnp.d()])
            k.dma("sync", out=c.x1d[r0:r0 + 128, :], in_=xo.h[:, b, :], r=[xo.d(b)], w=[], semdep=xo.d(b))
            for kc in range(8):
                k.op("tensor", lambda e: e.matmul(px.h[:, kc, :], lhsT=xo.h[:, b, kc * 128:(kc + 1) * 128],
                                                  rhs=c.identf.h[:], is_transpose=True),
                     r=[xo.d(b)], w=[px.d()])
            k.op("scalar", lambda e: e.copy(out=x1T.h[:, :, i * 128:(i + 1) * 128], in_=px.h[:]), r=[px.d()], w=[])
        k.end_phase()


def phase_C(c, l, s, dst, dst_dep, x1T):
    k, nc = c.k, c.nc
    pf = c.pfm
    with ExitStack() as st:
        wd = sb(st, nc, "pc_wd", [128, NFC, D], BF16, NFC)
        k.dma_batch("gpsimd", [(wd.h[:, kc, :], c.w_down[l, kc * 128:(kc + 1) * 128, :], wd.d(kc)) for kc in range(NFC)])
        lnp = sb(st, nc, "pc_lnp", [128, 2048], F32)
        k.dma("sync", out=lnp.h[:], in_=c.pbc[l:l + 1, LN2G:LN2G + 2048].partition_broadcast(128), w=[lnp.d()])
        hid = sb(st, nc, "pc_hid", [128, NFC, 1024], BF16, NFC)
        wup = sb(st, nc, "pc_wup", [128, 2, 2, 8, 256], BF16, 4)
        ua = sb(st, nc, "pc_ua", [128, 2, 1026], F32, 2)
        cv = sb(st, nc, "pc_cv", [128, 2, 1024], F32, 2)
        halo = sb(st, nc, "pc_halo", [128, NFC, 2], F32, NFC)
        xt = sb(st, nc, "pc_xt", [128, 2, D], F32, 2)
        tt = sb(st, nc, "pc_tt", [128, 2, D], F32, 2)
        sc = dict(stats=sb(st, nc, "pc_stats", [128, 2, 12], F32, 2), mv=sb(st, nc, "pc_mv", [128, 2, 2], F32, 2),
                  rs=sb(st, nc, "pc_rs", [128, 2, 2], F32, 2))
        P = [ps(st, nc, "pc_P%d" % i, [128, 1024], F32, 2) for i in range(4)]
        w_up_v = c.w_up[l].rearrange("(kc p) c -> p kc c", p=128)
        it = 0
        gi = 0
        for hs in range(2):
            for g in range(NFC // 2):
                sl = gi % 2
                gi += 1
                for ag in range(2):
                    col0 = ag * DFF + g * 256
                    k.dma("gpsimd", out=wup.h[:, sl, ag, :, :], in_=w_up_v[:, :, col0:col0 + 256],
                          w=[wup.d(sl * 2 + ag)])
                for fi in range(2):
                    fc = g * 2 + fi
                    b = it % 2
                    it += 1
                    pa, pg = P[b * 2], P[b * 2 + 1]
                    for ag, pp in ((0, pa), (1, pg)):
                        for tb in range(2):
                            for kc in range(8):
                                t0 = hs * 1024 + tb * 512
                                k.op("tensor", lambda e: e.matmul(
                                    pp.h[:, tb * 512:(tb + 1) * 512], lhsT=wup.h[:, sl, ag, kc, fi * 128:(fi + 1) * 128],
                                    rhs=x1T.h[:, kc, t0:t0 + 512], start=(kc == 0), stop=(kc == 7)),
                                    r=[wup.d(sl * 2 + ag)], w=[pp.d(tb)])
                    k.op("scalar", lambda e: e.copy(out=ua.h[:, b, 2:1026], in_=pa.h[:]), r=pa.all(), w=[ua.d(b)])
                    if hs == 0:
                        k.op("gpsimd", lambda e: e.memset(ua.h[:, b, 0:2], 0.0), w=[ua.d(b)])
                        k.op("gpsimd", lambda e: e.tensor_copy(out=halo.h[:, fc, :], in_=ua.h[:, b, 1024:1026]),
                             r=[ua.d(b)], w=[halo.d(fc)])
                    else:
                        k.op("gpsimd", lambda e: e.tensor_copy(out=ua.h[:, b, 0:2], in_=halo.h[:, fc, :]),
                             r=[halo.d(fc)], w=[ua.d(b)])
                    k.op("vector", lambda e: e.tensor_scalar(
                        out=cv.h[:, b, :], in0=ua.h[:, b, 2:1026], scalar1=pf.h[:, CWC + 2 * NFC + fc:CWC + 2 * NFC + fc + 1],
                        scalar2=pf.h[:, CBC + fc:CBC + fc + 1], op0=ALU.mult, op1=ALU.add), r=[ua.d(b)], w=[cv.d(b)])
                    k.op("vector", lambda e: e.scalar_tensor_tensor(
                        out=cv.h[:, b, :], in0=ua.h[:, b, 1:1025], scalar=pf.h[:, CWC + NFC + fc:CWC + NFC + fc + 1],
                        in1=cv.h[:, b, :], op0=ALU.mult, op1=ALU.add), r=[ua.d(b)], w=[cv.d(b)])
                    k.op("vector", lambda e: e.scalar_tensor_tensor(
                        out=cv.h[:, b, :], in0=ua.h[:, b, 0:1024], scalar=pf.h[:, CWC + fc:CWC + fc + 1],
                        in1=cv.h[:, b, :], op0=ALU.mult, op1=ALU.add), r=[ua.d(b)], w=[cv.d(b)])
                    k.op("scalar", lambda e: e.activation(out=cv.h[:, b, :], in_=cv.h[:, b, :], func=AF.Gelu),
                         r=[cv.d(b)], w=[cv.d(b)])
                    k.op("vector", lambda e: e.tensor_tensor(out=hid.h[:, fc, :], in0=cv.h[:, b, :], in1=pg.h[:],
                                                             op=ALU.mult), r=[cv.d(b)] + pg.all(), w=[hid.d(fc)])
            for j in range(8):
                i = hs * 8 + j
                b = j % 2
                pd = P[b]
                r0 = s * T + i * 128
                k.dma("sync", out=xt.h[:, b, :], in_=c.x1d[r0:r0 + 128, :], w=[xt.d(b)])
                for hf in range(2):
                    for kc in range(NFC):
                        k.op("tensor", lambda e: e.matmul(pd.h[:, hf * 512:(hf + 1) * 512],
                                                          lhsT=hid.h[:, kc, j * 128:(j + 1) * 128],
                                                          rhs=wd.h[:, kc, hf * 512:(hf + 1) * 512],
                                                          start=(kc == 0), stop=(kc == NFC - 1)),
                             r=[hid.d(kc), wd.d(kc)], w=[pd.d(hf)])
                for hf in range(2):
                    k.op("vector", lambda e: e.scalar_tensor_tensor(
                        out=tt.h[:, b, hf * 512:(hf + 1) * 512], in0=xt.h[:, b, hf * 512:(hf + 1) * 512], scalar=ALPHA,
                        in1=pd.h[:, hf * 512:(hf + 1) * 512], op0=ALU.mult, op1=ALU.add),
                        r=[xt.d(b), pd.d(hf)], w=[tt.d(b)])
                layernorm(c, tt.h[:, b, :], [tt.d(b)], tt.h[:, b, :], [tt.d(b)], lnp.h[:, 0:1024], lnp.h[:, 1024:2048], sc, b, [lnp.d()])
                k.dma("sync", out=dst[r0:r0 + 128, :], in_=tt.h[:, b, :], r=[tt.d(b)], w=[], semdep=tt.d(b))
        k.end_phase()


def attention(c, mode, qsel, ksel, kdim, scale, v, mix, mixcol):
    k, nc = c.k, c.nc
    R = 3
    with ExitStack() as st:
        wb = sb(st, nc, "at_wb", [128, R, T], BF16, R)
        wT = sb(st, nc, "at_wT", [128, R, T], BF16, R)
        if mode == "sb":
            e_sb = sb(st, nc, "at_e", [128, R, T], F32, R)
            Fb = sb(st, nc, "at_F", [128, R, T + 1], F32, R)
            ones = sb(st, nc, "at_ones", [128, T], F32)
            k.op("gpsimd", lambda e: e.memset(ones.h[:], 1.0), w=[ones.d()])
        else:
            mx = sb(st, nc, "at_mx", [128, R, 4], F32, R)
            maskb = sb(st, nc, "at_maskb", [128, 128], BF16)
            k.op("vector", lambda e: e.tensor_copy(out=maskb.h[:], in_=c.cst.h[:, C_TRILI:C_TRILI + 128]), w=[maskb.d()])
        pz = ps(st, nc, "at_pz", [128, T], F32, 4)
        pT = ps(st, nc, "at_pT", [128, NT, 128], BF16)
        po = [ps(st, nc, "at_po%d" % i, [128, 512], F32) for i in range(2)]
        NIT = 4 * NT

        def params(it):
            h, qb = it // NT, it % NT
            return h, qb, it % R, (qb + 1) * 128, qb * 128

        def s1(it):
            h, qb, b, nk, d0 = params(it)
            nb4 = (nk + 511) // 512
            for kb4 in range(nb4):
                n0 = kb4 * 512
                n1 = min(nk, n0 + 512)
                k.op("tensor", lambda e: e.matmul(pz.h[:, n0:n1], lhsT=qsel(h, d0, d0 + 128), rhs=ksel(h, n0, n1),
                                                  start=True, stop=True), w=[pz.d(kb4)])
            pzd = [pz.d(j) for j in range(nb4)]
            if mode == "sb":
                k.op("scalar", lambda e: e.activation(out=e_sb.h[:, b, 0:nk], in_=pz.h[:, 0:nk], func=AF.Exp, scale=scale),
                     r=pzd, w=[e_sb.d(b)])
                k.op("gpsimd", lambda e: e.tensor_tensor(out=e_sb.h[:, b, d0:nk], in0=e_sb.h[:, b, d0:nk],
                                                         in1=c.cst.h[:, C_TRILS:C_TRILS + 128], op=ALU.mult),
                     r=[e_sb.d(b)], w=[e_sb.d(b)])
                k.op("scalar", lambda e: e.activation(out=Fb.h[:, b, 1:nk + 1], in_=e_sb.h[:, b, 0:nk], func=AF.Ln, bias=1.0),
                     r=[e_sb.d(b)], w=[Fb.d(b)])
            else:
                k.op("vector", lambda e: e.tensor_reduce(out=mx.h[:, b, 0:1], in_=pz.h[:, 0:nk], axis=AX.X, op=ALU.max),
                     r=pzd, w=[mx.d(b)])
                k.op("gpsimd", lambda e: e.tensor_scalar(out=mx.h[:, b, 1:2], in0=mx.h[:, b, 0:1], scalar1=-scale, scalar2=None,
                                                         op0=ALU.mult), r=[mx.d(b)], w=[mx.d(b)])
                k.op("scalar", lambda e: e.activation(out=wb.h[:, b, 0:nk], in_=pz.h[:, 0:nk], func=AF.Exp, scale=scale,
                                                      bias=mx.h[:, b, 1:2]), r=pzd + [mx.d(b)], w=[wb.d(b)])
                k.op("gpsimd", lambda e: e.tensor_tensor(out=wb.h[:, b, d0:nk], in0=wb.h[:, b, d0:nk], in1=maskb.h[:],
                                                         op=ALU.mult), r=[wb.d(b), maskb.d()], w=[wb.d(b)])

        def s2(it):
            h, qb, b, nk, d0 = params(it)
            if mode == "sb":
                k.op("vector", lambda e: e.tensor_tensor_scan(out=Fb.h[:, b, 1:nk + 1], data0=ones.h[:, 0:nk],
                                                              data1=Fb.h[:, b, 1:nk + 1], initial=0.0,
                                                              op0=ALU.mult, op1=ALU.subtract),
                     r=[ones.d(), Fb.d(b)], w=[Fb.d(b)])
                k.op("gpsimd", lambda e: e.memset(Fb.h[:, b, 0:1], 0.0), r=[Fb.d(b)], w=[Fb.d(b)])
                k.op("scalar", lambda e: e.activation(out=Fb.h[:, b, 0:nk], in_=Fb.h[:, b, 0:nk], func=AF.Exp, scale=-1.0,
                                                      bias=Fb.h[:, b, nk:nk + 1]),
                     r=[Fb.d(b)], w=[Fb.d(b)])
                k.op("vector", lambda e: e.tensor_tensor(out=wb.h[:, b, 0:nk], in0=e_sb.h[:, b, 0:nk], in1=Fb.h[:, b, 0:nk],
                                                         op=ALU.mult), r=[e_sb.d(b), Fb.d(b)], w=[wb.d(b)])
            else:
                k.op("vector", lambda e: e.tensor_reduce(out=mx.h[:, b, 2:3], in_=wb.h[:, b, 0:nk], axis=AX.X, op=ALU.add),
                     r=[wb.d(b)], w=[mx.d(b)])
                k.op("vector", lambda e: e.reciprocal(out=mx.h[:, b, 3:4], in_=mx.h[:, b, 2:3]), r=[mx.d(b)], w=[mx.d(b)])
            for kb in range(qb + 1):
                k.op("tensor", lambda e: e.matmul(pT.h[:, kb, :], lhsT=wb.h[:, b, kb * 128:(kb + 1) * 128], rhs=c.identb.h[:],
                                                  is_transpose=True), r=[wb.d(b)], w=[pT.d()])
            if it % 2:
                k.op("scalar", lambda e: e.copy(out=wT.h[:, b, 0:nk], in_=pT.h[:, 0:qb + 1, :]), r=[pT.d()], w=[wT.d(b)])
            else:
                k.op("vector", lambda e: e.tensor_copy(out=wT.h[:, b, 0:nk], in_=pT.h[:, 0:qb + 1, :]), r=[pT.d()], w=[wT.d(b)])

        def s3(it):
            h, qb, b, nk, d0 = params(it)
            osl = it % 2
            for kb in range(qb + 1):
                k.op("tensor", lambda e: e.matmul(po[osl].h[:, 0:64], lhsT=wT.h[:, b, kb * 128:(kb + 1) * 128],
                                                  rhs=v.h[:, kb, h * 64:(h + 1) * 64], start=(kb == 0), stop=(kb == qb)),
                     r=[wT.d(b)], w=[po[osl].d()])
            mc = mixcol + h * 64
            if mode == "sb":
                k.op("scalar", lambda e: e.copy(out=mix.h[:, qb, mc:mc + 64], in_=po[osl].h[:, 0:64]), r=[po[osl].d()], w=[])
            else:
                k.op("vector", lambda e: e.tensor_scalar(out=mix.h[:, qb, mc:mc + 64], in0=po[osl].h[:, 0:64],
                                                         scalar1=mx.h[:, b, 3:4], scalar2=None, op0=ALU.mult),
                     r=[po[osl].d(), mx.d(b)], w=[])

        for step in range(NIT + 2):
            if step < NIT:
                s1(step)
            if 0 <= step - 1 < NIT:
                s2(step - 1)
            if 0 <= step - 2 < NIT:
                s3(step - 2)
        k.end_phase()


def proj_fm(c, pp, n, w_ap_fn, M, xT, out_fn, r=()):
    k = c.k
    for tb in range(4):
        p = pp[n[0] % len(pp)]
        n[0] += 1
        for kc in range(8):
            k.op("tensor", lambda e: e.matmul(p.h[0:M, :], lhsT=w_ap_fn(kc), rhs=xT.h[:, kc, tb * 512:(tb + 1) * 512],
                                              start=(kc == 0), stop=(kc == 7)), r=list(r), w=[p.d()])
        out_fn(tb, p)


def evac(c, n, out_ap, in_ap, r, w=()):
    k = c.k
    if n % 2:
        k.op("scalar", lambda e: e.copy(out=out_ap, in_=in_ap), r=r, w=list(w))
    else:
        k.op("vector", lambda e: e.tensor_copy(out=out_ap, in_=in_ap), r=r, w=list(w))


def mixer_sb(c, l, s, xT, mix):
    k, nc = c.k, c.nc
    with ExitStack() as st:
        w = sb(st, nc, "sb_w", [128, 8, 768], BF16)
        w_in_v = c.w_in[l].rearrange("(kc p) c -> p kc c", p=128)
        k.dma("gpsimd", out=w.h[:], in_=w_in_v[:, :, 0:768], w=[w.d()])
        qk = sb(st, nc, "sb_qk", [128, 4, T], BF16)
        v = sb(st, nc, "sb_v", [128, NT, 256], BF16)
        with ExitStack() as st2:
            pp = [ps(st2, nc, "sb_pp%d" % i, [128, 512], F32) for i in range(4)]
            n = [0]
            for ci in range(4):
                proj_fm(c, pp, n, lambda kc: w.h[:, kc, ci * 128:(ci + 1) * 128], 128, xT,
                        lambda tb, p: evac(c, n[0], qk.h[:, ci, tb * 512:(tb + 1) * 512], p.h[:], [p.d()]), r=[w.d()])
            for i in range(NT):
                p = pp[n[0] % 4]
                n[0] += 1
                for kc in range(8):
                    k.op("tensor", lambda e: e.matmul(p.h[:, 0:256], lhsT=xT.h[:, kc, i * 128:(i + 1) * 128],
                                                      rhs=w.h[:, kc, 512:768], start=(kc == 0), stop=(kc == 7)),
                         r=[w.d()], w=[p.d()])
                evac(c, n[0], v.h[:, i, :], p.h[:, 0:256], [p.d()])
            k.end_phase()

        def qsel(h, a, b):
            po_ = (h % 2) * 64
            return qk.h[po_:po_ + 64, h // 2, a:b]

        def ksel(h, a, b):
            po_ = (h % 2) * 64
            return qk.h[po_:po_ + 64, 2 + h // 2, a:b]

        attention(c, "sb", qsel, ksel, 64, 0.125, v, mix, 0)


def mixer_mla(c, l, s, xT, mix):
    k, nc = c.k, c.nc
    pf = c.pfm
    with ExitStack() as st:
        w = sb(st, nc, "ml_w", [128, 8, 352], BF16)
        wsw = sb(st, nc, "ml_wsw", [128, 8, 96], BF16)
        wuq = sb(st, nc, "ml_wuq", [128, 2, 2, 384], BF16)
        wkv = sb(st, nc, "ml_wkv", [128, 2, 256], BF16)
        rope = sb(st, nc, "ml_rope", [128, 2, T], F32)
        w_in_v = c.w_in[l].rearrange("(kc p) c -> p kc c", p=128)
        k.dma("gpsimd", out=w.h[:], in_=w_in_v[:, :, 768:1120], w=[w.d()])
        k.dma("gpsimd", out=wsw.h[:], in_=c.wkr_sw[l].rearrange("(kc p) c -> p kc c", p=128), w=[wsw.d()])
        for a in range(2):
            k.dma("gpsimd", out=wuq.h[:, :, a, :], in_=c.wuq2[l, a].rearrange("(kc p) c -> p kc c", p=128), w=[wuq.d()])
        k.dma("gpsimd", out=wkv.h[:], in_=c.wukv2[l].rearrange("a p c -> p a c"), w=[wkv.d()])
        k.dma("sync", out=rope.h[:], in_=c.rope_d.rearrange("a p t -> p a t"), w=[rope.d()])
        qT = sb(st, nc, "ml_qT", [128, 4, T], BF16)
        kT = sb(st, nc, "ml_kT", [128, 4, T], BF16)
        v = sb(st, nc, "ml_v", [128, NT, 256], BF16)
        with ExitStack() as st2:
            cqg = sb(st2, nc, "ml_cqg", [128, 2, 2, 512], BF16, 2)
            sqq = sb(st2, nc, "ml_sqq", [128, 2, 2, 512], BF16, 2)
            ckvg = sb(st2, nc, "ml_ckvg", [128, 2, 512], BF16, 2)
            sqkv = sb(st2, nc, "ml_sqkv", [128, 2, 512], BF16, 2)
            rq = sb(st2, nc, "ml_rq", [128, 2, 3, 512], F32, 2)
            rkv = sb(st2, nc, "ml_rkv", [128, 2, 512], F32, 2)
            rkt = sb(st2, nc, "ml_rkt", [128, 2, 8], F32, 2)
            ta = sb(st2, nc, "ml_ta", [128, 2, 512], F32, 2)
            tb_ = sb(st2, nc, "ml_tb", [128, 2, 512], F32, 2)
            pp = [ps(st2, nc, "ml_pp%d" % i, [128, 512], F32) for i in range(8)]
            n = [0]

            def nextp():
                p = pp[n[0] % 8]
                n[0] += 1
                return p

            k.end_phase()
            for tb in range(4):
                b = tb % 2
                ts = slice(tb * 512, (tb + 1) * 512)
                specs = [(0, 128, 128), (128, 192, 64), (192, 320, 128)]
                for si, (c0, c1, M) in enumerate(specs):
                    p = nextp()
                    for kc in range(8):
                        k.op("tensor", lambda e: e.matmul(p.h[0:M, :], lhsT=w.h[:, kc, c0:c1], rhs=xT.h[:, kc, ts],
                                                          start=(kc == 0), stop=(kc == 7)), w=[p.d()])
                    if si < 2:
                        k.op("scalar", lambda e: e.activation(out=cqg.h[0:M, b, si, :], in_=p.h[0:M, :], func=AF.Identity,
                                                              scale=pf.h[0:M, QNC + si:QNC + si + 1]), r=[p.d()], w=[cqg.d(b)])
                        k.op("scalar", lambda e: e.activation(out=sqq.h[0:M, b, si, :], in_=p.h[0:M, :], func=AF.Square),
                             r=[p.d()], w=[sqq.d(b)])
                    else:
                        k.op("scalar", lambda e: e.activation(out=ckvg.h[:, b, :], in_=p.h[:], func=AF.Identity,
                                                              scale=pf.h[:, KVNC:KVNC + 1]), r=[p.d()], w=[ckvg.d(b)])
                        k.op("scalar", lambda e: e.activation(out=sqkv.h[:, b, :], in_=p.h[:], func=AF.Square),
                             r=[p.d()], w=[sqkv.d(b)])
                p = nextp()
                k.op("tensor", lambda e: e.matmul(p.h[:], lhsT=c.onesb.h[:], rhs=sqq.h[:, b, 0, :], start=True, stop=False),
                     r=[sqq.d(b)], w=[p.d()])
                k.op("tensor", lambda e: e.matmul(p.h[:], lhsT=c.onesb.h[0:64, :], rhs=sqq.h[0:64, b, 1, :], start=False, stop=True),
                     r=[sqq.d(b)], w=[p.d()])
                k.op("scalar", lambda e: e.activation(out=rq.h[:, b, 0, :], in_=p.h[:], func=AF.Ln, scale=1.0 / 192, bias=c.epsr.h[:, 0:1]),
                     r=[p.d()], w=[rq.d(b)])
                k.op("scalar", lambda e: e.activation(out=rq.h[:, b, 0, :], in_=rq.h[:, b, 0, :], func=AF.Exp, scale=-0.5),
                     r=[rq.d(b)], w=[rq.d(b)])
                for j in range(2):
                    k.op("gpsimd", lambda e: e.tensor_tensor(out=rq.h[64:96, b, 1 + j, :], in0=rq.h[64:96, b, 0, :],
                                                             in1=rope.h[64:96, j, ts], op=ALU.mult), r=[rq.d(b)], w=[rq.d(b)])
                p = nextp()
                k.op("tensor", lambda e: e.matmul(p.h[:], lhsT=c.onesb.h[:], rhs=sqkv.h[:, b, :], start=True, stop=True),
                     r=[sqkv.d(b)], w=[p.d()])
                k.op("scalar", lambda e: e.activation(out=rkv.h[:, b, :], in_=p.h[:], func=AF.Ln, scale=1.0 / 128, bias=c.epsr.h[:, 0:1]),
                     r=[p.d()], w=[rkv.d(b)])
                k.op("scalar", lambda e: e.activation(out=rkv.h[:, b, :], in_=rkv.h[:, b, :], func=AF.Exp, scale=-0.5),
                     r=[rkv.d(b)], w=[rkv.d(b)])
                p = nextp()
                for i4 in range(4):
                    k.op("tensor", lambda e: e.matmul(p.h[:, i4:i4 + 1], lhsT=sqkv.h[:, b, i4 * 128:(i4 + 1) * 128],
                                                      rhs=c.onesb.h[:, 0:1], start=True, stop=True), r=[sqkv.d(b)], w=[p.d()])
                k.op("scalar", lambda e: e.activation(out=rkt.h[:, b, 0:4], in_=p.h[:, 0:4], func=AF.Ln, scale=1.0 / 128,
                                                      bias=c.epsr.h[:, 0:1]), r=[p.d()], w=[rkt.d(b)])
                k.op("scalar", lambda e: e.activation(out=rkt.h[:, b, 4:8], in_=rkt.h[:, b, 0:4], func=AF.Exp, scale=-0.5),
                     r=[rkt.d(b)], w=[rkt.d(b)])
                for h in range(4):
                    pq = nextp()
                    pqs = nextp()
                    for a, pqq in ((0, pq), (1, pqs)):
                        k.op("tensor", lambda e: e.matmul(pqq.h[0:96, :], lhsT=wuq.h[:, 0, a, h * 96:(h + 1) * 96], rhs=cqg.h[:, b, 0, :],
                                                          start=True, stop=False), r=[cqg.d(b)], w=[pqq.d()])
                        k.op("tensor", lambda e: e.matmul(pqq.h[0:96, :], lhsT=wuq.h[0:64, 1, a, h * 96:(h + 1) * 96],
                                                          rhs=cqg.h[0:64, b, 1, :], start=False, stop=True), r=[cqg.d(b)], w=[pqq.d()])
                    k.op("vector", lambda e: e.tensor_tensor(out=qT.h[0:64, h, ts], in0=pq.h[0:64, :], in1=rq.h[0:64, b, 0, :],
                                                             op=ALU.mult), r=[pq.d(), rq.d(b)], w=[])
                    k.op("vector", lambda e: e.tensor_tensor(out=ta.h[64:96, h % 2, :], in0=pq.h[64:96, :], in1=rq.h[64:96, b, 1, :],
                                                             op=ALU.mult), r=[pq.d(), rq.d(b)], w=[ta.d(h % 2)])
                    k.op("vector", lambda e: e.tensor_tensor(out=tb_.h[64:96, h % 2, :], in0=pqs.h[64:96, :], in1=rq.h[64:96, b, 2, :],
                                                             op=ALU.mult), r=[pqs.d(), rq.d(b)], w=[tb_.d(h % 2)])
                    k.op("gpsimd", lambda e: e.tensor_tensor(out=qT.h[64:96, h, ts], in0=ta.h[64:96, h % 2, :], in1=tb_.h[64:96, h % 2, :],
                                                             op=ALU.add), r=[ta.d(h % 2), tb_.d(h % 2)], w=[])
                for h in range(4):
                    pk = nextp()
                    k.op("tensor", lambda e: e.matmul(pk.h[0:64, :], lhsT=wkv.h[:, 0, h * 64:(h + 1) * 64], rhs=ckvg.h[:, b, :],
                                                      start=True, stop=True), r=[ckvg.d(b)], w=[pk.d()])
                    k.op("vector", lambda e: e.tensor_tensor(out=kT.h[0:64, h, ts], in0=pk.h[0:64, :], in1=rkv.h[0:64, b, :],
                                                             op=ALU.mult), r=[pk.d(), rkv.d(b)], w=[])
                pkr = nextp()
                pkrs = nextp()
                for kc in range(8):
                    k.op("tensor", lambda e: e.matmul(pkr.h[0:96, :], lhsT=w.h[:, kc, 256:352], rhs=xT.h[:, kc, ts],
                                                      start=(kc == 0), stop=(kc == 7)), w=[pkr.d()])
                for kc in range(8):
                    k.op("tensor", lambda e: e.matmul(pkrs.h[0:96, :], lhsT=wsw.h[:, kc, :], rhs=xT.h[:, kc, ts],
                                                      start=(kc == 0), stop=(kc == 7)), w=[pkrs.d()])
                k.op("vector", lambda e: e.tensor_tensor(out=ta.h[64:96, 0, :], in0=pkr.h[64:96, :], in1=rope.h[64:96, 0, ts],
                                                         op=ALU.mult), r=[pkr.d()], w=[ta.d(0)])
                k.op("vector", lambda e: e.tensor_tensor(out=tb_.h[64:96, 0, :], in0=pkrs.h[64:96, :], in1=rope.h[64:96, 1, ts],
                                                         op=ALU.mult), r=[pkrs.d()], w=[tb_.d(0)])
                for h in range(4):
                    k.op("gpsimd", lambda e: e.tensor_tensor(out=kT.h[64:96, h, ts], in0=ta.h[64:96, 0, :], in1=tb_.h[64:96, 0, :],
                                                             op=ALU.add), r=[ta.d(0), tb_.d(0)], w=[])
                for i4 in range(4):
                    i = tb * 4 + i4
                    pv = nextp()
                    k.op("tensor", lambda e: e.matmul(pv.h[:, 0:256], lhsT=ckvg.h[:, b, i4 * 128:(i4 + 1) * 128], rhs=wkv.h[:, 1, :],
                                                      start=True, stop=True), r=[ckvg.d(b)], w=[pv.d()])
                    k.op("vector", lambda e: e.tensor_scalar(out=v.h[:, i, :], in0=pv.h[:, 0:256], scalar1=rkt.h[:, b, 4 + i4:5 + i4],
                                                             scalar2=None, op0=ALU.mult), r=[pv.d(), rkt.d(b)], w=[])
            k.end_phase()

        attention(c, "sm", lambda h, a, b: qT.h[0:96, h, a:b], lambda h, a, b: kT.h[0:96, h, a:b], 96, 96 ** -0.5, v, mix, 256)


def mixer_rwkv(c, l, s, xT, mix):
    k, nc = c.k, c.nc
    pf = c.pfm
    cst = c.cst
    with ExitStack() as st:
        w_in_v = c.w_in[l].rearrange("(kc p) c -> p kc c", p=128)
        wl = sb(st, nc, "rw_wl", [128, 8, 256], BF16)
        k.dma("gpsimd", out=wl.h[:], in_=w_in_v[:, :, RW0 + 1536:RW0 + 1792], w=[wl.d()])
        w2a2 = sb(st, nc, "rw_w2a2", [128, 512], BF16)
        k.dma("gpsimd", out=w2a2.h[:], in_=c.w2a2_d[l], w=[w2a2.d()])
        g2 = sb(st, nc, "rw_g2", [128, 512], BF16)
        k.dma("gpsimd", out=g2.h[:], in_=c.g2_d[l], w=[g2.d()])
        gnp = sb(st, nc, "rw_gnp", [128, 1024], F32)
        k.dma("sync", out=gnp.h[:], in_=c.pbc[l:l + 1, GNG:GNG + 1024].partition_broadcast(128), w=[gnp.d()])
        wp = sb(st, nc, "rw_wp", [128, 2, 3, 8, 128], BF16, 2)
        lw = sb(st, nc, "rw_lw", [128, T], BF16)
        sg = sb(st, nc, "rw_sg", [128, T], BF16)
        halo = sb(st, nc, "rw_halo", [128, 16], F32, 16)
        bdb = sb(st, nc, "rw_bdb", [128, 128], BF16)
        hsb = sb(st, nc, "rw_hsb", [128, 32], BF16)
        pfx = sb(st, nc, "rw_pfx", [128, 8], F32)
        mh8 = sb(st, nc, "rw_mh8", [128, 8], F32)
        k.op("vector", lambda e: e.tensor_copy(out=bdb.h[:], in_=cst.h[:, C_BD:C_BD + 128]), w=[bdb.d()])
        k.op("vector", lambda e: e.memset(hsb.h[:], 0.0), w=[hsb.d()])
        k.op("vector", lambda e: e.tensor_copy(out=hsb.h[:, 0:2], in_=cst.h[:, C_HS:C_HS + 2]), w=[hsb.d()])
        k.op("vector", lambda e: e.tensor_scalar(out=pfx.h[:, 0:4], in0=pf.h[:, W0C:W0C + 4], scalar1=-1.0, scalar2=None, op0=ALU.mult),
             w=[pfx.d()])
        k.op("vector", lambda e: e.tensor_scalar(out=pfx.h[:, 4:8], in0=pf.h[:, KAC:KAC + 4], scalar1=-1.0, scalar2=1.0,
                                                 op0=ALU.mult, op1=ALU.add), w=[pfx.d()])
        k.op("vector", lambda e: e.memset(mh8.h[:], -0.5), w=[mh8.d()])
        M4 = {}
        for nm_, col_ in (("trils", C_TRILS), ("trius", C_TRIUS), ("triui", C_TRIUI), ("bd", C_BD), ("id", C_ID)):
            M4[nm_] = sb(st, nc, "rw_m4" + nm_, [128, 4, 128], F32)
            for q_ in range(4):
                k.op("vector", lambda e: e.tensor_copy(out=M4[nm_].h[:, q_, :], in_=cst.h[:, col_:col_ + 128]), w=[M4[nm_].d()])


        raw = sb(st, nc, "rw_raw", [128, 2, 513], F32, 2)
        dtmp = sb(st, nc, "rw_dtmp", [128, 2, 512], F32, 2)
        f32names = ["R", "KX", "VX", "LWN", "CUM", "EP", "EM", "EPV", "A", "KK", "RN", "KM", "T1", "T2"]
        F = {nm: sb(st, nc, "rw_" + nm, [128, 512], F32) for nm in f32names}
        b16names = ["AT", "BT", "KT", "RT", "RK", "SQ"]
        Bb = {nm: sb(st, nc, "rw_" + nm, [128, 512], BF16) for nm in b16names}
        AT2 = sb(st, nc, "rw_AT2", [128, 2, 512], BF16)
        RT2 = sb(st, nc, "rw_RT2", [128, 2, 512], BF16)
        k.op("gpsimd", lambda e: e.memset(AT2.h[:], 0.0), w=[AT2.d()])
        k.op("gpsimd", lambda e: e.memset(RT2.h[:], 0.0), w=[RT2.d()])
        tokn = ["A_tok", "B_tok", "K_tok", "Vb_tok"]
        Tk = {nm: sb(st, nc, "rw_" + nm, [128, 4, 128], BF16) for nm in tokn}
        V_tok = sb(st, nc, "rw_V_tok", [128, 4, 128], F32)
        X = sb(st, nc, "rw_X", [128, 4, 4, 128], BF16, 4)
        Xt = sb(st, nc, "rw_Xt", [128, 4, 4, 128], BF16, 4)
        Pt = sb(st, nc, "rw_Pt", [128, 4, 4, 128], BF16, 4)
        LakT = sb(st, nc, "rw_LakT", [128, 2, 4, 128], BF16, 2)
        MrbT = sb(st, nc, "rw_MrbT", [128, 2, 4, 128], BF16, 2)
        MrkT = sb(st, nc, "rw_MrkT", [128, 2, 4, 128], BF16, 2)
        Zb = sb(st, nc, "rw_Zb", [128, 8, 64], BF16)
        U0b = sb(st, nc, "rw_U0b", [128, 8, 64], BF16)
        W1b = sb(st, nc, "rw_W1b", [128, 8, 64], BF16)
        MTb = sb(st, nc, "rw_MTb", [128, 4, 128], BF16)
        Ng = sb(st, nc, "rw_Ng", [128, 4, 128], F32, 4)
        RqT = sb(st, nc, "rw_RqT", [128, 512], BF16, 2)
        Y0 = sb(st, nc, "rw_Y0", [128, 4, 128], F32)
        yb = sb(st, nc, "rw_yb", [128, 4, 128], F32, 4)
        y2 = sb(st, nc, "rw_y2", [128, 4, 128], F32)
        bs = sb(st, nc, "rw_bs", [128, 8], F32)
        stt = sb(st, nc, "rw_stt", [128, 48], F32)
        Hb = sb(st, nc, "rw_Hb", [128, 2, 128], BF16, 2)
        pp = [ps(st, nc, "rw_pp%d" % i, [128, 512], F32) for i in range(7)]
        pdm = ps(st, nc, "rw_pdm", [128, 512], F32, 1)
        n = [0]
        ntb = [0]

        def nextp():
            p = pp[n[0] % 4]
            n[0] += 1
            return p

        rbc = [0]

        def reset():
            k.op("tensor", lambda e: e.matmul(pdm.h[:], lhsT=xT.h[:, 0, 0:128], rhs=xT.h[:, 0, 0:512], start=True, stop=True), w=[])

        def shifted_proj(wfn, j, tb, out, r=()):
            ts = slice(tb * 512, (tb + 1) * 512)
            p = nextp()
            for kc in range(8):
                k.op("tensor", lambda e: e.matmul(p.h[:], lhsT=wfn(kc), rhs=xT.h[:, kc, ts], start=(kc == 0), stop=(kc == 7)),
                     r=list(r), w=[p.d()])
            rb = rbc[0] % 2
            rbc[0] += 1
            k.op("scalar", lambda e: e.copy(out=raw.h[:, rb, 1:513], in_=p.h[:]), r=[p.d()], w=[raw.d(rb)])
            if tb == 0:
                k.op("gpsimd", lambda e: e.memset(raw.h[:, rb, 0:1], 0.0), w=[raw.d(rb)])
            else:
                k.op("gpsimd", lambda e: e.tensor_copy(out=raw.h[:, rb, 0:1], in_=halo.h[:, j:j + 1]), r=[halo.d(j)], w=[raw.d(rb)])
            k.op("gpsimd", lambda e: e.tensor_copy(out=halo.h[:, j:j + 1], in_=raw.h[:, rb, 512:513]), r=[raw.d(rb)], w=[halo.d(j)])
            k.op("vector", lambda e: e.tensor_tensor(out=dtmp.h[:, rb, :], in0=raw.h[:, rb, 0:512], in1=raw.h[:, rb, 1:513],
                                                     op=ALU.subtract), r=[raw.d(rb)], w=[dtmp.d(rb)])
            k.op("vector", lambda e: e.scalar_tensor_tensor(out=out.h[:], in0=dtmp.h[:, rb, :], scalar=pf.h[:, MU0 + j:MU0 + j + 1],
                                                            in1=raw.h[:, rb, 1:513], op0=ALU.mult, op1=ALU.add),
                 r=[dtmp.d(rb), raw.d(rb)], w=[out.d()])

        for tb in range(4):
            ts = slice(tb * 512, (tb + 1) * 512)
            shifted_proj(lambda kc: wl.h[:, kc, 0:128], 12, tb, F["T1"], r=[wl.d()])
            k.op("scalar", lambda e: e.activation(out=lw.h[0:64, ts], in_=F["T1"].h[0:64, :], func=AF.Tanh), r=[F["T1"].d()], w=[lw.d()])
            k.op("scalar", lambda e: e.copy(out=lw.h[64:128, ts], in_=F["T1"].h[64:128, :]), r=[F["T1"].d()], w=[lw.d()])
            shifted_proj(lambda kc: wl.h[:, kc, 128:256], 13, tb, F["T2"], r=[wl.d()])
            k.op("scalar", lambda e: e.activation(out=sg.h[:, ts], in_=F["T2"].h[:], func=AF.Sigmoid), r=[F["T2"].d()], w=[sg.d()])

        if c.rw_stop == 0:
            k.end_phase()
            return
        def load_wp(hp):
            sl = hp % 2
            k.dma_batch("gpsimd", [(wp.h[:, sl, a, :, :], w_in_v[:, :, RW0 + a * 512 + hp * 128:RW0 + a * 512 + (hp + 1) * 128], wp.d(sl))
                                   for a in range(3)])

        def bc4(ap):
            return ap.unsqueeze(1).to_broadcast([128, 4, 128])

        load_wp(0)
        for hp in range(4):
            if hp + 1 < 4:
                load_wp(hp + 1)
            wsl = hp % 2
            k.op("gpsimd", lambda e: e.memset(Hb.h[:, 0, :], 0.0), w=[Hb.d(0)])
            hcur = 0
            for tb in range(4):
                ts = slice(tb * 512, (tb + 1) * 512)
                shifted_proj(lambda kc: wp.h[:, wsl, 0, kc, :], hp, tb, F["R"], r=[wp.d(wsl)])
                shifted_proj(lambda kc: wp.h[:, wsl, 1, kc, :], 4 + hp, tb, F["KX"], r=[wp.d(wsl)])
                shifted_proj(lambda kc: wp.h[:, wsl, 2, kc, :], 8 + hp, tb, F["VX"], r=[wp.d(wsl)])
                p = nextp()
                k.op("tensor", lambda e: e.matmul(p.h[:], lhsT=w2a2.h[0:64, hp * 128:(hp + 1) * 128], rhs=lw.h[0:64, ts],
                                                  start=True, stop=True), r=[lw.d(), w2a2.d()], w=[p.d()])
                k.op("scalar", lambda e: e.activation(out=F["LWN"].h[:], in_=p.h[:], func=AF.Exp, scale=-1.0, bias=pfx.h[:, hp:hp + 1]),
                     r=[p.d(), pfx.d()], w=[F["LWN"].d()])
                k.op("scalar", lambda e: e.activation(out=F["LWN"].h[:], in_=F["LWN"].h[:], func=AF.Ln, bias=1.0),
                     r=[F["LWN"].d()], w=[F["LWN"].d()])
                k.op("scalar", lambda e: e.activation(out=F["LWN"].h[:], in_=F["LWN"].h[:], func=AF.Exp, scale=-1.0, bias=c.epsr.h[:, 1:2]),
                     r=[F["LWN"].d()], w=[F["LWN"].d()])
                k.op("vector", lambda e: e.tensor_tensor_scan(out=F["CUM"].h[:], data0=cst.h[:, C_RM:C_RM + 512], data1=F["LWN"].h[:],
                                                              initial=0.0, op0=ALU.mult, op1=ALU.subtract),
                     r=[F["LWN"].d()], w=[F["CUM"].d()])
                k.op("scalar", lambda e: e.activation(out=F["EP"].h[:], in_=F["CUM"].h[:], func=AF.Exp), r=[F["CUM"].d()], w=[F["EP"].d()])
                k.op("scalar", lambda e: e.activation(out=F["EM"].h[:], in_=F["CUM"].h[:], func=AF.Exp, scale=-1.0),
                     r=[F["CUM"].d()], w=[F["EM"].d()])
                k.op("gpsimd", lambda e: e.tensor_tensor(out=F["EPV"].h[:], in0=F["CUM"].h[:], in1=F["LWN"].h[:], op=ALU.add),
                     r=[F["CUM"].d(), F["LWN"].d()], w=[F["EPV"].d()])
                k.op("scalar", lambda e: e.activation(out=F["EPV"].h[:], in_=F["EPV"].h[:], func=AF.Exp), r=[F["EPV"].d()], w=[F["EPV"].d()])
                p = nextp()
                k.op("tensor", lambda e: e.matmul(p.h[:], lhsT=w2a2.h[64:128, hp * 128:(hp + 1) * 128], rhs=lw.h[64:128, ts],
                                                  start=True, stop=True), r=[lw.d(), w2a2.d()], w=[p.d()])
                k.op("scalar", lambda e: e.activation(out=F["A"].h[:], in_=p.h[:], func=AF.Sigmoid, bias=pf.h[:, A0C + hp:A0C + hp + 1]),
                     r=[p.d()], w=[F["A"].d()])
                k.op("vector", lambda e: e.tensor_scalar(out=F["KK"].h[:], in0=F["KX"].h[:], scalar1=pf.h[:, KKC + hp:KKC + hp + 1],
                                                         scalar2=None, op0=ALU.mult), r=[F["KX"].d()], w=[F["KK"].d()])
                k.op("scalar", lambda e: e.activation(out=Bb["SQ"].h[:], in_=F["KK"].h[:], func=AF.Square), r=[F["KK"].d()], w=[Bb["SQ"].d()])
                p = nextp()
                k.op("tensor", lambda e: e.matmul(p.h[:], lhsT=bdb.h[:], rhs=Bb["SQ"].h[:], start=True, stop=True),
                     r=[Bb["SQ"].d(), bdb.d()], w=[p.d()])
                k.op("vector", lambda e: e.tensor_scalar(out=F["RN"].h[:], in0=p.h[:], scalar1=1e-12, scalar2=None, op0=ALU.max),
                     r=[p.d()], w=[F["RN"].d()])
                k.op("scalar", lambda e: e.activation(out=F["RN"].h[:], in_=F["RN"].h[:], func=AF.Ln), r=[F["RN"].d()], w=[F["RN"].d()])
                k.op("scalar", lambda e: e.activation(out=F["RN"].h[:], in_=F["RN"].h[:], func=AF.Exp, scale=-0.5),
                     r=[F["RN"].d()], w=[F["RN"].d()])
                k.op("gpsimd", lambda e: e.tensor_tensor(out=F["KK"].h[:], in0=F["KK"].h[:], in1=F["RN"].h[:], op=ALU.mult),
                     r=[F["KK"].d(), F["RN"].d()], w=[F["KK"].d()])
                k.op("gpsimd", lambda e: e.tensor_scalar(out=F["KM"].h[:], in0=F["A"].h[:], scalar1=pf.h[:, KAC + hp:KAC + hp + 1],
                                                         scalar2=pfx.h[:, 4 + hp:5 + hp], op0=ALU.mult, op1=ALU.add),
                     r=[F["A"].d(), pfx.d()], w=[F["KM"].d()])
                k.op("gpsimd", lambda e: e.tensor_tensor(out=F["KM"].h[:], in0=F["KM"].h[:], in1=F["KX"].h[:], op=ALU.mult),
                     r=[F["KM"].d(), F["KX"].d()], w=[F["KM"].d()])
                k.op("gpsimd", lambda e: e.tensor_tensor(out=F["A"].h[:], in0=F["A"].h[:], in1=F["KK"].h[:], op=ALU.mult),
                     r=[F["A"].d(), F["KK"].d()], w=[F["A"].d()])
                k.op("vector", lambda e: e.scalar_tensor_tensor(out=Bb["AT"].h[:], in0=F["KK"].h[:], scalar=-1.0, in1=F["EPV"].h[:],
                                                                op0=ALU.mult, op1=ALU.mult), r=[F["KK"].d(), F["EPV"].d()], w=[Bb["AT"].d()])
                k.op("vector", lambda e: e.tensor_tensor(out=Bb["BT"].h[:], in0=F["A"].h[:], in1=F["EM"].h[:], op=ALU.mult),
                     r=[F["A"].d(), F["EM"].d()], w=[Bb["BT"].d()])
                k.op("vector", lambda e: e.tensor_tensor(out=Bb["KT"].h[:], in0=F["KM"].h[:], in1=F["EM"].h[:], op=ALU.mult),
                     r=[F["KM"].d(), F["EM"].d()], w=[Bb["KT"].d()])
                k.op("gpsimd", lambda e: e.tensor_tensor(out=Bb["RT"].h[:], in0=F["R"].h[:], in1=F["EP"].h[:], op=ALU.mult),
                     r=[F["R"].d(), F["EP"].d()], w=[Bb["RT"].d()])
                k.op("vector", lambda e: e.scalar_tensor_tensor(out=Bb["RK"].h[:], in0=F["R"].h[:], scalar=pf.h[:, RKC + hp:RKC + hp + 1],
                                                                in1=F["KM"].h[:], op0=ALU.mult, op1=ALU.mult),
                     r=[F["R"].d(), F["KM"].d()], w=[Bb["RK"].d()])
                if c.rw_stop == 1 or (hp * 4 + tb) >= int(os.environ.get("RW_NB", "99")):
                    continue
                for j_ in range(2):
                    pr_ = slice(j_ * 64, (j_ + 1) * 64)
                    k.op("scalar", lambda e: e.copy(out=AT2.h[pr_, j_, :], in_=Bb["AT"].h[pr_, :]), r=[Bb["AT"].d()], w=[AT2.d()])
                    k.op("gpsimd", lambda e: e.tensor_copy(out=RT2.h[pr_, j_, :], in_=Bb["RT"].h[pr_, :]), r=[Bb["RT"].d()], w=[RT2.d()])
                T2 = os.environ.get("T2", "abvr")
                if "x" in T2:
                    k.barrier()
                if "v" in T2:
                    if "y" in T2:
                        k.barrier()
                    p = pp[6]
                    for ci in range(4):
                        k.op("tensor", lambda e: e.matmul(p.h[:, ci * 128:(ci + 1) * 128], lhsT=F["VX"].h[:, ci * 128:(ci + 1) * 128],
                                                          rhs=c.identf.h[:], start=True, stop=True), r=[F["VX"].d()], w=[p.d()])
                    k.op("scalar", lambda e: e.copy(out=V_tok.h[:], in_=p.h[:].rearrange("p (c t) -> p c t", c=4)), r=[p.d()], w=[V_tok.d()])
                    k.op("vector", lambda e: e.tensor_copy(out=Tk["Vb_tok"].h[:], in_=p.h[:].rearrange("p (c t) -> p c t", c=4)),
                         r=[p.d()], w=[Tk["Vb_tok"].d()])
                    reset()
                alist = (("A_tok", Bb["AT"]), ("B_tok", Bb["BT"]), ("K_tok", Bb["KT"]))
                if "1" in T2:
                    alist = alist[0:1]
                if "2" in T2:
                    alist = alist[0:2]
                for nm, src_b in (alist if "a" in T2 else ()):
                    if "y" in T2:
                        k.barrier()
                    p = pp[4 + ntb[0] % 2]
                    ntb[0] += 1
                    for ci in range(4):
                        k.op("tensor", lambda e: e.matmul(p.h[:, ci * 128:(ci + 1) * 128], lhsT=src_b.h[:, ci * 128:(ci + 1) * 128],
                                                          rhs=c.identb.h[:], start=True, stop=True), r=[src_b.d()], w=[p.d()])
                    if "d" in T2:
                        k.op("vector", lambda e: e.tensor_copy(out=Tk[nm].h[:], in_=p.h[:].rearrange("p (c t) -> p c t", c=4)), r=[p.d()], w=[Tk[nm].d()])
                    else:
                        k.op("scalar", lambda e: e.copy(out=Tk[nm].h[:], in_=p.h[:].rearrange("p (c t) -> p c t", c=4)), r=[p.d()], w=[Tk[nm].d()])
                        reset()
                if "y" in T2:
                    k.barrier()
                p = nextp()
                for ci in range(4 if "b" in T2 else 0):
                    k.op("tensor", lambda e: e.matmul(p.h[:, ci * 32:(ci + 1) * 32], lhsT=Bb["RK"].h[:, ci * 128:(ci + 1) * 128], rhs=hsb.h[:],
                                                      start=True, stop=True), r=[Bb["RK"].d(), hsb.d()], w=[p.d()])
                k.op("scalar", lambda e: e.copy(out=bs.h[:].rearrange("p (c n) -> p c n", n=2),
                                                in_=p.h[:, 0:128].rearrange("p (c n) -> p c n", n=32)[:, :, 0:2]), r=[p.d()], w=[bs.d()])
                reset()

                if c.rw_stop == 2:
                    continue

                def unit(u):
                    ci, j = u // 2, u % 2
                    return ci, j, slice(j * 64, (j + 1) * 64), slice(ci * 128, (ci + 1) * 128)

                for g in range(2):
                    pL, pLT = nextp(), nextp()
                    for uu in range(4):
                        ci, j, pr, cs = unit(g * 4 + uu)
                        k.op("tensor", lambda e: e.matmul(pL.h[:, uu * 128:(uu + 1) * 128], lhsT=AT2.h[:, j, cs], rhs=Bb["BT"].h[:, cs],
                                                          start=True, stop=True), r=[AT2.d(), Bb["BT"].d()], w=[pL.d()])
                        k.op("tensor", lambda e: e.matmul(pLT.h[:, uu * 128:(uu + 1) * 128], lhsT=Bb["BT"].h[:, cs], rhs=AT2.h[:, j, cs],
                                                          start=True, stop=True), r=[AT2.d(), Bb["BT"].d()], w=[pLT.d()])
                    k.op("vector", lambda e: e.tensor_tensor(out=X.h[:, g, :, :], in0=pL.h[:].rearrange("p (c t) -> p c t", c=4),
                                                             in1=M4["trils"].h[:], op=ALU.mult), r=[pL.d()], w=[X.d(g)])
                    k.op("vector", lambda e: e.tensor_tensor(out=Xt.h[:, g, :, :], in0=pLT.h[:].rearrange("p (c t) -> p c t", c=4),
                                                             in1=M4["trius"].h[:], op=ALU.mult), r=[pLT.d()], w=[Xt.d(g)])
                    k.op("vector", lambda e: e.tensor_tensor(out=Pt.h[:, g, :, :], in0=Xt.h[:, g, :, :], in1=M4["id"].h[:],
                                                             op=ALU.add), r=[Xt.d(g)], w=[Pt.d(g)])
                    reset()
                if c.rw_stop == 3:
                    continue
                for lev in range(1, 7):
                    cur, nxt = (lev - 1) % 2, lev % 2
                    for g in range(2):
                        sc_, sn_ = cur * 2 + g, nxt * 2 + g
                        pX = nextp()
                        for uu in range(4):
                            k.op("tensor", lambda e: e.matmul(pX.h[:, uu * 128:(uu + 1) * 128], lhsT=Xt.h[:, sc_, uu, :], rhs=X.h[:, sc_, uu, :],
                                                              start=True, stop=True), r=[Xt.d(sc_), X.d(sc_)], w=[pX.d()])
                        if lev < 6:
                            pXt = nextp()
                            for uu in range(4):
                                k.op("tensor", lambda e: e.matmul(pXt.h[:, uu * 128:(uu + 1) * 128], lhsT=X.h[:, sc_, uu, :],
                                                                  rhs=Xt.h[:, sc_, uu, :], start=True, stop=True),
                                     r=[Xt.d(sc_), X.d(sc_)], w=[pXt.d()])
                        k.op("scalar", lambda e: e.copy(out=X.h[:, sn_, :, :], in_=pX.h[:].rearrange("p (c t) -> p c t", c=4)),
                             r=[pX.d()], w=[X.d(sn_)])
                        if lev < 6:
                            k.op("vector", lambda e: e.tensor_copy(out=Xt.h[:, sn_, :, :], in_=pXt.h[:].rearrange("p (c t) -> p c t", c=4)),
                                 r=[pXt.d()], w=[Xt.d(sn_)])
                        pP = nextp()
                        for uu in range(4):
                            k.op("tensor", lambda e: e.matmul(pP.h[:, uu * 128:(uu + 1) * 128], lhsT=X.h[:, sn_, uu, :], rhs=Pt.h[:, sc_, uu, :],
                                                              start=True, stop=True), r=[X.d(sn_), Pt.d(sc_)], w=[pP.d()])
                        k.op("vector", lambda e: e.tensor_tensor(out=Pt.h[:, sn_, :, :], in0=pP.h[:].rearrange("p (c t) -> p c t", c=4),
                                                                 in1=Pt.h[:, sc_, :, :], op=ALU.add), r=[pP.d(), Pt.d(sc_)], w=[Pt.d(sn_)])
                        reset()
                TTs = 0

                def TT(u):
                    return Pt.h[:, TTs * 2 + u // 4, u % 4, :]

                if c.rw_stop == 4:
                    continue
                for g in range(2):
                    pA, pB, pC = nextp(), nextp(), nextp()
                    for uu in range(4):
                        ci, j, pr, cs = unit(g * 4 + uu)
                        us = slice(uu * 128, (uu + 1) * 128)
                        k.op("tensor", lambda e: e.matmul(pA.h[:, us], lhsT=Bb["KT"].h[:, cs], rhs=AT2.h[:, j, cs], start=True, stop=True),
                             r=[Bb["KT"].d(), AT2.d()], w=[pA.d()])
                        k.op("tensor", lambda e: e.matmul(pB.h[:, us], lhsT=Bb["BT"].h[:, cs], rhs=RT2.h[:, j, cs], start=True, stop=True),
                             r=[Bb["BT"].d(), RT2.d()], w=[pB.d()])
                        k.op("tensor", lambda e: e.matmul(pC.h[:, us], lhsT=Bb["KT"].h[:, cs], rhs=RT2.h[:, j, cs], start=True, stop=True),
                             r=[Bb["KT"].d(), RT2.d()], w=[pC.d()])
                    for pq_, dst_, mk in ((pA, LakT, "trius"), (pB, MrbT, "triui"), (pC, MrkT, "triui")):
                        k.op("vector", lambda e: e.tensor_tensor(out=dst_.h[:, g, :, :], in0=pq_.h[:].rearrange("p (c t) -> p c t", c=4),
                                                                 in1=M4[mk].h[:], op=ALU.mult), r=[pq_.d()], w=[dst_.d(g)])
                        reset()
                if c.rw_stop == 5:
                    continue
                pZ = nextp()
                for u in range(8):
                    ci, j, pr, cs = unit(u)
                    k.op("tensor", lambda e: e.matmul(pZ.h[:, u * 64:(u + 1) * 64], lhsT=LakT.h[:, u // 4, u % 4, :], rhs=Tk["Vb_tok"].h[:, ci, pr],
                                                      start=True, stop=True), r=[LakT.d(u // 4), Tk["Vb_tok"].d()], w=[pZ.d()])
                k.op("scalar", lambda e: e.copy(out=Zb.h[:], in_=pZ.h[:].rearrange("p (u v) -> p u v", u=8)), r=[pZ.d()], w=[Zb.d()])
                reset()
                pU, pW = nextp(), nextp()
                for u in range(8):
                    ci, j, pr, cs = unit(u)
                    k.op("tensor", lambda e: e.matmul(pU.h[:, u * 64:(u + 1) * 64], lhsT=TT(u), rhs=Zb.h[:, u, :], start=True, stop=True),
                         r=[Pt.d(TTs * 2 + u // 4), Zb.d()], w=[pU.d()])
                    k.op("tensor", lambda e: e.matmul(pW.h[:, u * 64:(u + 1) * 64], lhsT=TT(u), rhs=Tk["A_tok"].h[:, ci, pr], start=True, stop=True),
                         r=[Pt.d(TTs * 2 + u // 4), Tk["A_tok"].d()], w=[pW.d()])
                k.op("scalar", lambda e: e.copy(out=U0b.h[:], in_=pU.h[:].rearrange("p (u v) -> p u v", u=8)), r=[pU.d()], w=[U0b.d()])
                k.op("vector", lambda e: e.tensor_copy(out=W1b.h[:], in_=pW.h[:].rearrange("p (u v) -> p u v", u=8)), r=[pW.d()], w=[W1b.d()])
                reset()
                if c.rw_stop == 6:
                    continue
                pM, pN = nextp(), nextp()
                for ci in range(4):
                    cs = slice(ci * 128, (ci + 1) * 128)
                    w1p = W1b.h[:, ci * 2:ci * 2 + 2, :].rearrange("p a v -> p (a v)")
                    u0p = U0b.h[:, ci * 2:ci * 2 + 2, :].rearrange("p a v -> p (a v)")
                    k.op("tensor", lambda e: e.matmul(pM.h[:, cs], lhsT=w1p, rhs=Tk["B_tok"].h[:, ci, :], start=True, stop=False),
                         r=[W1b.d(), Tk["B_tok"].d()], w=[pM.d()])
                    k.op("tensor", lambda e: e.matmul(pM.h[:, cs], lhsT=c.identb.h[:], rhs=c.identb.h[:], start=False, stop=True), w=[pM.d()])
                    k.op("tensor", lambda e: e.matmul(pN.h[:, cs], lhsT=Tk["B_tok"].h[:, ci, :], rhs=u0p, start=True, stop=False),
                         r=[U0b.d(), Tk["B_tok"].d()], w=[pN.d()])
                    k.op("tensor", lambda e: e.matmul(pN.h[:, cs], lhsT=Tk["K_tok"].h[:, ci, :], rhs=Tk["Vb_tok"].h[:, ci, :], start=False, stop=True),
                         r=[Tk["K_tok"].d(), Tk["Vb_tok"].d()], w=[pN.d()])
                k.op("vector", lambda e: e.tensor_tensor(out=MTb.h[:], in0=pM.h[:].rearrange("p (c t) -> p c t", c=4),
                                                         in1=M4["bd"].h[:], op=ALU.mult), r=[pM.d()], w=[MTb.d()])
                reset()
                for ci in range(4):
                    cs = slice(ci * 128, (ci + 1) * 128)
                    k.op("vector", lambda e: e.scalar_tensor_tensor(out=Ng.h[:, ci, :], in0=pN.h[:, cs], scalar=F["EP"].h[:, ci * 128 + 127:ci * 128 + 128],
                                                                    in1=cst.h[:, C_BD:C_BD + 128], op0=ALU.mult, op1=ALU.mult),
                         r=[pN.d(), F["EP"].d()], w=[Ng.d(ci)])
                if c.rw_stop == 7:
                    continue
                for j in range(2):
                    pR = nextp()
                    pr = slice(j * 64, (j + 1) * 64)
                    for ci in range(4):
                        u = ci * 2 + j
                        w1p = W1b.h[:, ci * 2:ci * 2 + 2, :].rearrange("p a v -> p (a v)")
                        k.op("tensor", lambda e: e.matmul(pR.h[:, ci * 128:(ci + 1) * 128], lhsT=w1p, rhs=MrbT.h[:, u // 4, u % 4, :],
                                                          start=True, stop=True), r=[W1b.d(), MrbT.d(u // 4)], w=[pR.d()])
                    k.op("vector", lambda e: e.tensor_tensor(out=RqT.h[pr, :], in0=pR.h[pr, :], in1=Bb["RT"].h[pr, :], op=ALU.add),
                         r=[pR.d(), Bb["RT"].d()], w=[RqT.d(j)])
                    reset()
                pY0 = nextp()
                for u in range(8):
                    ci, j, pr, cs = unit(u)
                    k.op("tensor", lambda e: e.matmul(pY0.h[:, u * 64:(u + 1) * 64], lhsT=MrbT.h[:, u // 4, u % 4, :], rhs=U0b.h[:, u, :],
                                                      start=True, stop=False), r=[MrbT.d(u // 4), U0b.d()], w=[pY0.d()])
                    k.op("tensor", lambda e: e.matmul(pY0.h[:, u * 64:(u + 1) * 64], lhsT=MrkT.h[:, u // 4, u % 4, :], rhs=Tk["Vb_tok"].h[:, ci, pr],
                                                      start=False, stop=True), r=[MrkT.d(u // 4), Tk["Vb_tok"].d()], w=[pY0.d()])
                k.op("scalar", lambda e: e.copy(out=Y0.h[:], in_=pY0.h[:].rearrange("p (c t) -> p c t", c=4)), r=[pY0.d()], w=[Y0.d()])
                reset()
                if c.rw_stop == 8:
                    continue
                for ci in range(4):
                    cs = slice(ci * 128, (ci + 1) * 128)
                    pY, pH = nextp(), nextp()
                    k.op("tensor", lambda e: e.matmul(pY.h[:, 0:128], lhsT=RqT.h[:, cs], rhs=Hb.h[:, hcur, :], start=True, stop=True),
                         r=[RqT.d(0), RqT.d(1), Hb.d(hcur)], w=[pY.d()])
                    k.op("tensor", lambda e: e.matmul(pH.h[:, 0:128], lhsT=MTb.h[:, ci, :], rhs=Hb.h[:, hcur, :], start=True, stop=True),
                         r=[MTb.d(), Hb.d(hcur)], w=[pH.d()])
                    k.op("vector", lambda e: e.scalar_tensor_tensor(out=Hb.h[:, 1 - hcur, :], in0=pH.h[:, 0:128],
                                                                    scalar=F["EP"].h[:, ci * 128 + 127:ci * 128 + 128], in1=Ng.h[:, ci, :],
                                                                    op0=ALU.mult, op1=ALU.add),
                         r=[pH.d(), Ng.d(ci), F["EP"].d()], w=[Hb.d(1 - hcur)])
                    k.op("vector", lambda e: e.tensor_tensor(out=yb.h[:, ci, :], in0=pY.h[:, 0:128], in1=Y0.h[:, ci, :], op=ALU.add),
                         r=[pY.d(), Y0.d()], w=[yb.d(ci)])
                    reset()
                    hcur = 1 - hcur
                if c.rw_stop == 9:
                    continue
                ybd = yb.all()
                y8 = yb.h[:].rearrange("p c (j v) -> p (c j) v", j=2)
                y28 = y2.h[:].rearrange("p c (j v) -> p (c j) v", j=2)
                v8 = V_tok.h[:].rearrange("p c (j v) -> p (c j) v", j=2)

                def b64(ap):
                    return ap.unsqueeze(2).to_broadcast([128, 8, 64])

                k.op("vector", lambda e: e.tensor_reduce(out=stt.h[:, 0:8], in_=y8, axis=AX.X, op=ALU.add), r=ybd, w=[stt.d()])
                k.op("scalar", lambda e: e.activation(out=y2.h[:], in_=yb.h[:], func=AF.Square), r=ybd, w=[y2.d()])
                k.op("vector", lambda e: e.tensor_reduce(out=stt.h[:, 8:16], in_=y28, axis=AX.X, op=ALU.add), r=[y2.d()], w=[stt.d()])
                k.op("vector", lambda e: e.tensor_scalar(out=stt.h[:, 16:24], in0=stt.h[:, 0:8], scalar1=1.0 / 64, scalar2=None, op0=ALU.mult),
                     r=[stt.d()], w=[stt.d()])
                k.op("vector", lambda e: e.tensor_tensor(out=stt.h[:, 24:32], in0=stt.h[:, 16:24], in1=stt.h[:, 16:24], op=ALU.mult),
                     r=[stt.d()], w=[stt.d()])
                k.op("vector", lambda e: e.scalar_tensor_tensor(out=stt.h[:, 32:40], in0=stt.h[:, 8:16], scalar=1.0 / 64, in1=stt.h[:, 24:32],
                                                                op0=ALU.mult, op1=ALU.subtract), r=[stt.d()], w=[stt.d()])
                k.op("gpsimd", lambda e: e.tensor_scalar(out=stt.h[:, 32:40], in0=stt.h[:, 32:40], scalar1=GN_EPS, scalar2=None, op0=ALU.add),
                     r=[stt.d()], w=[stt.d()])
                k.op("gpsimd", lambda e: e.tensor_tensor(out=stt.h[:, 40:48], in0=stt.h[:, 32:40], in1=mh8.h[:], op=ALU.pow),
                     r=[stt.d(), mh8.d()], w=[stt.d()])
                gcol = slice(hp * 128, (hp + 1) * 128)
                for u in range(8):
                    ci, j = u // 2, u % 2
                    ysl = yb.h[:, ci, j * 64:(j + 1) * 64]
                    k.op("vector", lambda e: e.tensor_scalar(out=ysl, in0=ysl, scalar1=stt.h[:, 16 + u:17 + u], scalar2=stt.h[:, 40 + u:41 + u],
                                                             op0=ALU.subtract, op1=ALU.mult), r=[stt.d()] + ybd, w=ybd)
                for ci in range(4):
                    k.op("gpsimd", lambda e: e.tensor_tensor(out=yb.h[:, ci, :], in0=yb.h[:, ci, :], in1=gnp.h[:, gcol], op=ALU.mult),
                         r=[gnp.d()] + ybd, w=ybd)
                    k.op("gpsimd", lambda e: e.tensor_tensor(out=yb.h[:, ci, :], in0=yb.h[:, ci, :],
                                                             in1=gnp.h[:, 512 + hp * 128:512 + (hp + 1) * 128], op=ALU.add),
                         r=[gnp.d()] + ybd, w=ybd)
                for u in range(8):
                    ci, j = u // 2, u % 2
                    ysl = yb.h[:, ci, j * 64:(j + 1) * 64]
                    k.op("vector", lambda e: e.scalar_tensor_tensor(out=ysl, in0=V_tok.h[:, ci, j * 64:(j + 1) * 64], scalar=bs.h[:, u:u + 1],
                                                                    in1=ysl, op0=ALU.mult, op1=ALU.add), r=[V_tok.d(), bs.d()] + ybd, w=ybd)
                pG = nextp()
                for ci in range(4):
                    t0 = tb * 512 + ci * 128
                    k.op("tensor", lambda e: e.matmul(pG.h[:, ci * 128:(ci + 1) * 128], lhsT=sg.h[:, t0:t0 + 128], rhs=g2.h[:, gcol],
                                                      start=True, stop=True), r=[sg.d(), g2.d()], w=[pG.d()])
                mc = 512 + hp * 128
                k.op("vector", lambda e: e.tensor_tensor(out=mix.h[:, tb * 4:(tb + 1) * 4, mc:mc + 128], in0=yb.h[:],
                                                         in1=pG.h[:].rearrange("p (c t) -> p c t", c=4), op=ALU.mult), r=[pG.d()] + ybd, w=[])
        k.end_phase()


def phase_A(c, l, s, src, mix, mixers):
    k, nc = c.k, c.nc
    if not mixers:
        return
    with ExitStack() as st:
        xT = sb(st, nc, "xT", [128, 8, T], BF16)
        phase_xT(c, src, s, xT)
        if "sb" in mixers:
            mixer_sb(c, l, s, xT, mix)
        if "mla" in mixers:
            mixer_mla(c, l, s, xT, mix)
        if "rwkv" in mixers:
            mixer_rwkv(c, l, s, xT, mix)


def build(nl, mixers=("sb", "mla", "rwkv"), dbg_mix=False):
    nc = bass.Bass("TRN2", target_bir_lowering=False)
    c = Ctx()
    c.nc = nc
    c.k = k = K(nc)
    import os
    c.rw_stop = int(os.environ.get('RW_STOP', '99'))
    dt = nc.dram_tensor
    x_in = dt("x", [NSEQ * T, D], F32, kind="ExternalInput").ap()
    y_out = dt("y", [NSEQ * T, D], F32, kind="ExternalOutput").ap()
    c.w_in = dt("w_in", [nl, D, IN_COLS], F32, kind="ExternalInput").ap()
    c.w_o = dt("w_o", [nl, D, D], F32, kind="ExternalInput").ap()
    c.w_up = dt("w_up", [nl, D, 2 * DFF], F32, kind="ExternalInput").ap()
    c.w_down = dt("w_down", [nl, DFF, D], F32, kind="ExternalInput").ap()
    c.pfm_d = dt("pfm", [nl, 128, NPF], F32, kind="ExternalInput").ap()
    c.pbc = dt("pbc", [nl, NPB], F32, kind="ExternalInput").ap()
    c.cst_d = dt("cst", [128, NCST], F32, kind="ExternalInput").ap()
    c.wkr_sw = dt("wkr_sw", [nl, D, 96], F32, kind="ExternalInput").ap()
    c.wuq2 = dt("wuq2", [nl, 2, 256, 384], F32, kind="ExternalInput").ap()
    c.wukv2 = dt("wukv2", [nl, 2, 128, 256], F32, kind="ExternalInput").ap()
    c.rope_d = dt("rope", [2, 128, T], F32, kind="ExternalInput").ap()
    c.w2a2_d = dt("w2a2", [nl, 128, 512], F32, kind="ExternalInput").ap()
    c.g2_d = dt("g2", [nl, 128, 512], F32, kind="ExternalInput").ap()
    c.x1d = dt("x1d", [NSEQ * T, D], F32).ap()
    c.x1d_dep = Dep(True)
    scr = [dt("xs0", [NSEQ * T, D], F32).ap(), dt("xs1", [NSEQ * T, D], F32).ap()]
    scr_dep = [Dep(True), Dep(True)]
    ydep = Dep(True)
    if dbg_mix:
        c.mix_d = dt("mixd", [NSEQ * T, D], BF16, kind="ExternalOutput").ap()
        c.mix_dep = Dep(True)

    asb = nc.alloc_sbuf_tensor
    c.cst = Buf(asb("cst_sb", [128, NCST], F32), 1, True)
    c.identf = Buf(c.cst.h[:, C_ID:C_ID + 128])
    c.identb = Buf(asb("identb", [128, 128], BF16))
    c.mhalf = Buf(asb("mhalf", [128, 1], F32))
    c.pfm = Buf(asb("pfm_sb", [128, NPF], F32), 1, True)

    k.dma("sync", out=c.cst.h[:], in_=c.cst_d, w=[c.cst.d()])
    k.op("vector", lambda e: e.tensor_copy(out=c.identb.h[:], in_=c.cst.h[:, C_ID:C_ID + 128]), r=[c.cst.d()], w=[c.identb.d()])
    k.op("vector", lambda e: e.memset(c.mhalf.h[:], -0.5), w=[c.mhalf.d()])
    c.onesb = Buf(asb("onesb", [128, 128], BF16))
    k.op("vector", lambda e: e.memset(c.onesb.h[:], 1.0), w=[c.onesb.d()])
    c.epsr = Buf(asb("epsr", [128, 2], F32))
    k.op("vector", lambda e: e.memset(c.epsr.h[:, 0:1], RMS_EPS), w=[c.epsr.d()])
    k.op("vector", lambda e: e.memset(c.epsr.h[:, 1:2], -0.5), w=[c.epsr.d()])
    k.end_phase()

    for l in range(nl):
        src = x_in if l == 0 else scr[(l - 1) % 2]
        dst, dst_dep = (y_out, ydep) if l == nl - 1 else (scr[l % 2], scr_dep[l % 2])
        k.dma("sync", out=c.pfm.h[:], in_=c.pfm_d[l], w=[c.pfm.d()])
        k.end_phase()
        for s in range(NSEQ):
            st_mix = ExitStack()
            mix = Buf(st_mix.enter_context(nc.sbuf_tensor(_uname("mix"), [128, NT, D], BF16, side="right")))
            k.op("gpsimd", lambda e: e.memset(mix.h[:], 0.0))
            k.end_phase()
            phase_A(c, l, s, src, mix, mixers)
            if dbg_mix and l == nl - 1:
                for i in range(NT):
                    k.dma("sync", out=c.mix_d[s * T + i * 128:s * T + (i + 1) * 128, :], in_=mix.h[:, i, :],
                          w=[c.mix_dep], nowaw=True)
                k.end_phase()
            with ExitStack() as st_x1:
                x1T = sb(st_x1, nc, "x1T", [128, 8, T], BF16)
                phase_B(c, l, s, src, mix, x1T)
                st_mix.close()
                phase_C(c, l, s, dst, dst_dep, x1T)
    k.end_phase()
    return nc


def make_consts():
    cst = np.zeros((128, NCST), np.float32)
    p = np.arange(128)
    cst[:, C_ID:C_ID + 128] = np.eye(128, dtype=np.float32)
    cst[:, C_TRILS:C_TRILS + 128] = (p[None, :] < p[:, None])
    cst[:, C_TRIUS:C_TRIUS + 128] = (p[:, None] < p[None, :])
    cst[:, C_TRILI:C_TRILI + 128] = (p[None, :] <= p[:, None])
    cst[:, C_TRIUI:C_TRIUI + 128] = (p[:, None] <= p[None, :])
    cst[:, C_BD:C_BD + 128] = ((p[:, None] // 64) == (p[None, :] // 64))
    cst[:, C_HS] = (p // 64 == 0)
    cst[:, C_HS + 1] = (p // 64 == 1)
    cst[:, C_RM:C_RM + 512] = (np.arange(512) % 128 != 0)[None, :]
    return cst


def make_rope():
    inv_freq = (1.0 / (np.float32(10000.0) ** (np.arange(0, 32, 2, dtype=np.float32) / np.float32(32)))).astype(np.float32)
    ang = np.arange(T, dtype=np.float32)[:, None] * inv_freq[None, :]
    cos, sin = np.cos(ang).astype(np.float32), np.sin(ang).astype(np.float32)
    r = np.zeros((2, 128, T), np.float32)
    r[0, 64:80] = cos.T
    r[0, 80:96] = cos.T
    r[1, 64:80] = -sin.T
    r[1, 80:96] = sin.T
    return r


def colvec(v, n):
    return np.ascontiguousarray(v.reshape(n, 128).T)


def pack_params(inp, l):
    pfm = np.zeros((128, NPF), np.float32)
    pfm[:, MU0:MU0 + 14] = colvec(inp["rwkv_mu"][l], 14)
    pfm[:, W0C:W0C + 4] = colvec(inp["rwkv_w0"][l], 4)
    pfm[:, A0C:A0C + 4] = colvec(inp["rwkv_a0"][l], 4)
    pfm[:, KKC:KKC + 4] = colvec(inp["rwkv_k_k"][l], 4)
    pfm[:, KAC:KAC + 4] = colvec(inp["rwkv_k_a"][l], 4)
    pfm[:, RKC:RKC + 4] = colvec(inp["rwkv_r_k"][l].reshape(-1), 4)
    qn = np.zeros(256, np.float32)
    qn[:192] = inp["mla_q_norm"][l]
    pfm[:, QNC:QNC + 2] = colvec(qn, 2)
    pfm[:, KVNC:KVNC + 1] = colvec(inp["mla_kv_norm"][l], 1)
    for j in range(3):
        pfm[:, CWC + j * NFC:CWC + (j + 1) * NFC] = colvec(inp["ffn_conv_w"][l, j], NFC)
    pfm[:, CBC:CBC + NFC] = colvec(inp["ffn_conv_b"][l], NFC)
    pbc = np.concatenate([inp["ln1_g"][l], inp["ln1_b"][l], inp["ln2_g"][l], inp["ln2_b"][l],
                          inp["rwkv_gn_g"][l], inp["rwkv_gn_b"][l]]).astype(np.float32)
    return pfm, pbc


_PROG = {}


def get_prog(nl, **kw):
    key = (nl, tuple(sorted(kw.items())))
    if key not in _PROG:
        _PROG[key] = build(nl, **kw)
    return _PROG[key]


def host_inputs(inp, layers):
    f = lambda a: np.ascontiguousarray(np.asarray(a, dtype=np.float32))
    packed = [pack_params(inp, l) for l in layers]
    shared = {
        "w_in": f(inp["w_in"][layers]),
        "w_o": f(inp["w_o"][layers]),
        "w_up": f(inp["ffn_w_up"][layers]),
        "w_down": f(inp["ffn_w_down"][layers]),
        "pfm": f(np.stack([p[0] for p in packed])),
        "pbc": f(np.stack([p[1] for p in packed])),
        "cst": make_consts(),
        "rope": make_rope(),
    }
    perm = np.concatenate([np.arange(16, 32), np.arange(0, 16)])
    wkr, wuq2, wukv2 = [], [], []
    for l in layers:
        w_in = np.asarray(inp["w_in"][l], np.float32)
        wkr.append(np.concatenate([w_in[:, 1024:1088], w_in[:, 1088:1120][:, perm]], axis=1))
        wq = np.asarray(inp["mla_w_uq"][l], np.float32)
        wqs = wq.copy().reshape(192, 4, 96)
        wqs[:, :, 64:96] = wqs[:, :, 64:96][:, :, perm]
        z = np.zeros((2, 256, 384), np.float32)
        z[0, :192] = wq
        z[1, :192] = wqs.reshape(192, 384)
        wuq2.append(z)
        wkv = np.asarray(inp["mla_w_ukv"][l], np.float32).reshape(128, 4, 128)
        wukv2.append(np.stack([wkv[:, :, 0:64].reshape(128, 256), wkv[:, :, 64:128].reshape(128, 256)]))
    shared["w2a2"] = f(np.stack([np.concatenate([inp["rwkv_w2"][l], inp["rwkv_a2"][l]], axis=0) for l in layers]))
    shared["g2"] = f(np.stack([inp["rwkv_g2"][l] for l in layers]))
    shared["wkr_sw"] = f(np.stack(wkr))
    shared["wuq2"] = f(np.stack(wuq2))
    shared["wukv2"] = f(np.stack(wukv2))
    return shared


def kernel(**inputs):
    inp = {kk: np.asarray(v) for kk, v in inputs.items()}
    x = np.ascontiguousarray(inp["x"], dtype=np.float32)
    B = x.shape[0]
    per = B // NCORES
    nc = get_prog(DEPTH)
    shared = host_inputs(inp, list(range(DEPTH)))
    in_maps = []
    for ci in range(NCORES):
        m = dict(shared)
        m["x"] = np.ascontiguousarray(x[ci * per:(ci + 1) * per].reshape(per * T, D))
        in_maps.append(m)
    res = run_bass_kernel_spmd(nc, in_maps, core_ids=list(range(NCORES)))
    out = np.concatenate([r["y"].reshape(per, T, D) for r in res.results], axis=0)
    return out.astype(np.float32)
```
